# Optimizing a Trainium2 kernel written in Bass

```python
import jax, jax.numpy as jnp
from jax import lax
import numpy as np

D_MODEL = 1024
BATCH = 8
SEQ = 2048
DEPTH = 2
DEC_BATCH = 128
DEC_SEQ = 8
PAST_LEN = 16384
PAGE_SIZE = 128

SGU_HEADS = 8
SGU_HEAD_DIM = D_MODEL // 16
D_SGU = SGU_HEADS * SGU_HEAD_DIM
CHUNK = 128
D_LRU = D_MODEL
LRU_BLOCKS = 16
LRU_BLOCK_DIM = D_LRU // LRU_BLOCKS
CONV_W = 4
LRU_C = 8.0
D_MIX = D_SGU + D_LRU
D_IN = 2 * D_SGU + 2 * D_LRU
D_FF = 4 * D_MODEL
EPS = 1e-6

kernel_name = "hymba_sgu_rglru_decoder_step"


def rms_norm(x, g):
    x32 = x.astype(jnp.float32)
    y = x32 * lax.rsqrt(jnp.mean(x32 * x32, axis=-1, keepdims=True) + EPS)
    return (y * g.astype(jnp.float32)).astype(x.dtype)


def layer_norm(x, g, b):
    x32 = x.astype(jnp.float32)
    mu = jnp.mean(x32, axis=-1, keepdims=True)
    xc = x32 - mu
    y = xc * lax.rsqrt(jnp.mean(xc * xc, axis=-1, keepdims=True) + EPS)
    return (y * g.astype(jnp.float32) + b.astype(jnp.float32)).astype(x.dtype)


def causal_conv(x, buf, w, b):
    T = x.shape[1]
    xp = jnp.concatenate([buf.astype(x.dtype), x], axis=1)
    y = b + sum(xp[:, k:k + T] * w[k] for k in range(CONV_W))
    return y, xp[:, T:]


def spatial_gate(v, w_s, b_s):
    B, T, H, dh = v.shape
    L = min(T, CHUNK)
    nc = T // L
    mask = jnp.tril(jnp.ones((L, L), w_s.dtype))
    w = w_s[:, :L, :L] * mask
    vc = v.reshape(B, nc, L, H, dh)
    out = jnp.einsum('hts,bcshd->bcthd', w, vc) + b_s[:, :L].T[None, None, :, :, None]
    return out.reshape(B, T, H, dh)


def rg_lru(x, h0, gate_r_w, gate_r_b, gate_i_w, gate_i_b, lam):
    B, T, _ = x.shape
    x32 = x.astype(jnp.float32)
    xh = x32.reshape(B, T, LRU_BLOCKS, LRU_BLOCK_DIM)
    r = jax.nn.sigmoid(jnp.einsum('bthi,hij->bthj', xh, gate_r_w.astype(jnp.float32)).reshape(B, T, D_LRU) + gate_r_b.astype(jnp.float32))
    i = jax.nn.sigmoid(jnp.einsum('bthi,hij->bthj', xh, gate_i_w.astype(jnp.float32)).reshape(B, T, D_LRU) + gate_i_b.astype(jnp.float32))
    log_a = LRU_C * r * jax.nn.log_sigmoid(lam.astype(jnp.float32))
    a = jnp.exp(log_a)
    bt = jnp.sqrt(-jnp.expm1(2.0 * log_a)) * (i * x32)
    bt = bt.at[:, 0].add(a[:, 0] * h0.astype(jnp.float32))

    def combine(c1, c2):
        a1, b1 = c1
        a2, b2 = c2
        return a1 * a2, a2 * b1 + b2

    _, h = lax.associative_scan(combine, (a, bt), axis=1)
    return h.astype(x.dtype), h[:, -1].astype(h0.dtype)


def mixer(xn, h0, conv_buf, w_in, conv_w, conv_b, gate_r_w, gate_r_b, gate_i_w, gate_i_b,
          lru_lambda, sgu_norm_g, sgu_norm_b, sgu_w, sgu_b, w_out):
    B, T, _ = xn.shape
    proj = xn @ w_in
    u, v, xb, yb = jnp.split(proj, [D_SGU, 2 * D_SGU, 2 * D_SGU + D_LRU], axis=-1)
    u = jax.nn.gelu(u)
    v = layer_norm(jax.nn.gelu(v), sgu_norm_g, sgu_norm_b)
    gate = spatial_gate(v.reshape(B, T, SGU_HEADS, SGU_HEAD_DIM), sgu_w, sgu_b).reshape(B, T, D_SGU)
    out_a = u * gate
    xc, conv_new = causal_conv(xb, conv_buf, conv_w, conv_b)
    h, h_last = rg_lru(xc, h0, gate_r_w, gate_r_b, gate_i_w, gate_i_b, lru_lambda)
    out_b = h * jax.nn.gelu(yb)
    out = jnp.concatenate([out_a, out_b], axis=-1) @ w_out
    return out, v, h_last, conv_new


def trunk(x, h_init, conv_init, norm_mix_g, w_in, conv_w, conv_b, gate_r_w, gate_r_b, gate_i_w, gate_i_b,
          lru_lambda, sgu_norm_g, sgu_norm_b, sgu_w, sgu_b, w_out, norm_mlp_g, mlp_w1, mlp_w2, final_norm_g):
    hs, convs, vs = [], [], []
    for l in range(DEPTH):
        mix, v, h_last, conv_new = mixer(
            rms_norm(x, norm_mix_g[l]), h_init[l], conv_init[l], w_in[l], conv_w[l], conv_b[l],
            gate_r_w[l], gate_r_b[l], gate_i_w[l], gate_i_b[l], lru_lambda[l],
            sgu_norm_g[l], sgu_norm_b[l], sgu_w[l], sgu_b[l], w_out[l])
        x = x + mix
        hid = jnp.square(jax.nn.relu(rms_norm(x, norm_mlp_g[l]) @ mlp_w1[l]))
        x = x + hid @ mlp_w2[l]
        hs.append(h_last)
        convs.append(conv_new)
        vs.append(v)
    return rms_norm(x, final_norm_g), hs, convs, vs


def setup_inputs(seed: int = 0) -> dict:
    key = jax.random.key(seed)
    ks = jax.random.split(key, 24)

    def nrm(k, shape, scale):
        return jax.random.normal(k, shape, jnp.float32) * scale

    u = jax.random.uniform(ks[12], (DEPTH, D_LRU), jnp.float32, minval=0.9, maxval=0.999)
    s = u ** (1.0 / LRU_C)
    lru_lambda = jnp.log(s) - jnp.log1p(-s)
    return {
        "x_prompt": nrm(ks[0], (BATCH, SEQ, D_MODEL), 1.0),
        "x_sample": nrm(ks[1], (DEC_BATCH, DEC_SEQ, D_MODEL), 1.0),
        "state_lru_h": nrm(ks[2], (DEPTH, DEC_BATCH, D_LRU), 0.5),
        "state_conv": nrm(ks[3], (DEPTH, DEC_BATCH, CONV_W - 1, D_LRU), 1.0),
        "norm_mix_g": 1.0 + nrm(ks[4], (DEPTH, D_MODEL), 0.02),
        "w_in": nrm(ks[5], (DEPTH, D_MODEL, D_IN), D_MODEL ** -0.5),
        "conv_w": nrm(ks[6], (DEPTH, CONV_W, D_LRU), CONV_W ** -0.5),
        "conv_b": nrm(ks[7], (DEPTH, D_LRU), 0.02),
        "gate_r_w": nrm(ks[8], (DEPTH, LRU_BLOCKS, LRU_BLOCK_DIM, LRU_BLOCK_DIM), LRU_BLOCK_DIM ** -0.5),
        "gate_r_b": nrm(ks[9], (DEPTH, D_LRU), 0.02),
        "gate_i_w": nrm(ks[10], (DEPTH, LRU_BLOCKS, LRU_BLOCK_DIM, LRU_BLOCK_DIM), LRU_BLOCK_DIM ** -0.5),
        "gate_i_b": nrm(ks[11], (DEPTH, D_LRU), 0.02),
        "lru_lambda": lru_lambda,
        "sgu_norm_g": 1.0 + nrm(ks[13], (DEPTH, D_SGU), 0.02),
        "sgu_norm_b": nrm(ks[14], (DEPTH, D_SGU), 0.02),
        "sgu_w": nrm(ks[15], (DEPTH, SGU_HEADS, CHUNK, CHUNK), CHUNK ** -0.5),
        "sgu_b": 1.0 + nrm(ks[16], (DEPTH, SGU_HEADS, CHUNK), 0.02),
        "w_out": nrm(ks[17], (DEPTH, D_MIX, D_MODEL), D_MIX ** -0.5),
        "norm_mlp_g": 1.0 + nrm(ks[18], (DEPTH, D_MODEL), 0.02),
        "mlp_w1": nrm(ks[19], (DEPTH, D_MODEL, D_FF), D_MODEL ** -0.5),
        "mlp_w2": nrm(ks[20], (DEPTH, D_FF, D_MODEL), D_FF ** -0.5),
        "final_norm_g": 1.0 + nrm(ks[21], (D_MODEL,), 0.02),
    }


def reference(x_prompt, x_sample, state_lru_h, state_conv, norm_mix_g, w_in, conv_w, conv_b,
              gate_r_w, gate_r_b, gate_i_w, gate_i_b, lru_lambda, sgu_norm_g, sgu_norm_b, sgu_w, sgu_b,
              w_out, norm_mlp_g, mlp_w1, mlp_w2, final_norm_g):
    b_p = x_prompt.shape[0]
    h0_p = jnp.zeros((DEPTH, b_p, D_LRU), state_lru_h.dtype)
    conv0_p = jnp.zeros((DEPTH, b_p, CONV_W - 1, D_LRU), state_conv.dtype)
    y_prompt, hs_p, convs_p, _ = trunk(
        x_prompt, h0_p, conv0_p, norm_mix_g, w_in, conv_w, conv_b, gate_r_w, gate_r_b, gate_i_w, gate_i_b,
        lru_lambda, sgu_norm_g, sgu_norm_b, sgu_w, sgu_b, w_out, norm_mlp_g, mlp_w1, mlp_w2, final_norm_g)
    y_sample, hs_s, convs_s, vs_s = trunk(
        x_sample, state_lru_h, state_conv, norm_mix_g, w_in, conv_w, conv_b, gate_r_w, gate_r_b, gate_i_w, gate_i_b,
        lru_lambda, sgu_norm_g, sgu_norm_b, sgu_w, sgu_b, w_out, norm_mlp_g, mlp_w1, mlp_w2, final_norm_g)
    return (y_prompt, y_sample, jnp.stack(hs_p), jnp.stack(convs_p), jnp.stack(hs_s), jnp.stack(convs_s), jnp.stack(vs_s))
```

```python
import contextlib

import numpy as np
import concourse.bass as bass
import concourse.mybir as mybir
from concourse.bass_utils import run_bass_kernel_spmd

F32 = mybir.dt.float32
BF16 = mybir.dt.bfloat16
AF = mybir.ActivationFunctionType
ALU = mybir.AluOpType

NCORES = 8
D = 1024
NCH = 8
DSGU = 512
DFF = 4096
DEPTH = 2
SEQ = 2048
EPS = 1e-6
TILES = [("p", 0, 512), ("p", 512, 512), ("p", 1024, 512), ("p", 1536, 512), ("s", 0, 128)]
GROUPS = [[0], [1], [2], [3, 4]]
NSLOT = 6
UNIT = 4096
N_WARM = 10

LV = 80
V_NMIX, V_NMLP, V_CW, V_CB, V_RB, V_IB, V_LAM = 0, 8, 16, 48, 56, 64, 72
V_FIN = 160
NV = 168


class Src:
    def __init__(self, name, sem):
        self.name = name
        self.sem = sem
        self.count = 0


class T:
    def __init__(self, name, handle, excl=False):
        self.name = name
        self.h = handle
        self.excl = excl
        self.w = None
        self.r = []

    def __getitem__(self, idx):
        return self.h[idx]


class Eng:
    def __init__(self, name, handle, src):
        self.name = name
        self.h = handle
        self.src = src
        self.known = {}


class K:
    def __init__(self, nc, stack):
        self.nc = nc
        self.stack = stack
        self.engs = {}
        for name, h in (("pe", nc.tensor), ("act", nc.scalar), ("dve", nc.vector),
                        ("pool", nc.gpsimd), ("sp", nc.sync)):
            self.engs[name] = Eng(name, h, Src(name, self._sem("e_" + name)))

    def _sem(self, name):
        return self.stack.enter_context(self.nc.semaphore(name))

    def dma_src(self, name):
        return Src(name, self._sem("d_" + name))

    def sb(self, name, shape, dtype):
        return T(name, self.stack.enter_context(self.nc.sbuf_tensor("s_" + name, list(shape), dtype)))

    def ps(self, name, shape, dtype):
        return T(name, self.stack.enter_context(self.nc.psum_tensor("p_" + name, list(shape), dtype)), excl=True)

    def _sync(self, e, reads, writes):
        deps = {}

        def add(sv):
            if sv is None:
                return
            s, v = sv
            if e.name == "pe" and s is e.src:
                return
            if deps.get(s, 0) < v:
                deps[s] = v

        for t in reads:
            add(t.w)
            if t.excl:
                for sv in t.r:
                    add(sv)
        for t in writes:
            add(t.w)
            for sv in t.r:
                add(sv)
        for s, v in deps.items():
            if e.known.get(s, 0) < v:
                e.h.wait_ge(s.sem, v)
                e.known[s] = v

    @staticmethod
    def _mark(tok, reads, writes):
        for t in reads:
            t.r.append(tok)
            if len(t.r) > 12:
                best = {}
                for s, v in t.r:
                    if best.get(s, 0) < v:
                        best[s] = v
                t.r = list(best.items())
        for t in writes:
            t.w = tok
            t.r = []

    def op(self, eng, fn, reads=(), writes=()):
        e = self.engs[eng]
        self._sync(e, reads, writes)
        ins = fn(e.h)
        e.src.count += 1
        ins.then_inc(e.src.sem, 1)
        tok = (e.src, e.src.count)
        self._mark(tok, reads, writes)
        return tok

    def dma(self, q, src, fn, reads=(), writes=()):
        e = self.engs[q]
        self._sync(e, reads, writes)
        inss = fn(e.h)
        if not isinstance(inss, (list, tuple)):
            inss = [inss]
        for ins in inss:
            src.count += 16
            ins.then_inc(src.sem, 16)
        tok = (src, src.count)
        self._mark(tok, reads, writes)
        return tok

    def wait_all(self, eng, toks):
        e = self.engs[eng]
        best = {}
        for s, v in toks:
            if best.get(s, 0) < v:
                best[s] = v
        for s, v in best.items():
            if e.known.get(s, 0) < v:
                e.h.wait_ge(s.sem, v)
                e.known[s] = v


class Rot:
    def __init__(self, items):
        self.items = items
        self.i = 0

    def next(self):
        t = self.items[self.i % len(self.items)]
        self.i += 1
        return t


def build_program(groups=GROUPS):
    nc = bass.Bass("TRN2", target_bir_lowering=False)

    def din(name, shape):
        return nc.dram_tensor(name, list(shape), F32, kind="ExternalInput").ap()

    def dout(name, shape):
        return nc.dram_tensor(name, list(shape), F32, kind="ExternalOutput").ap()

    xp_d = din("xp", [128, NCH, SEQ])
    xs_d = din("xs", [128, NCH, 128])
    h0s_d = din("h0s", [DEPTH, 128, NCH, 16])
    cv0s_d = din("cv0s", [DEPTH, 128, NCH, 16, 3])
    vecs_d = din("vecs", [128, NV])
    lnp_d = din("lnp", [DEPTH, 2, 128, DSGU])
    brow_d = din("brow", [DEPTH, 2, 40, 128])
    sel_d = din("sel40", [40, 8, 64])
    sw_d = din("sw", [DEPTH, 2, 128, 8, 128])
    gbd_d = din("gbd", [DEPTH, 2, 128, NCH, 128])
    w_in_d = din("w_in", [DEPTH, D, 3 * D])
    w_out_d = din("w_out", [DEPTH, 1536, D])
    w1_d = din("w1", [DEPTH, D, DFF])
    w2_d = din("w2", [DEPTH, DFF, D])

    yp_d = dout("yp", [128, NCH, SEQ])
    ys_d = dout("ys", [128, NCH, 128])
    nhp_d = dout("nhp", [DEPTH, 128, NCH])
    ncp_d = dout("ncp", [DEPTH, 128, NCH, 3])
    nhs_d = dout("nhs", [DEPTH, 128, NCH, 16])
    ncs_d = dout("ncs", [DEPTH, 128, NCH, 16, 3])
    nvs_d = dout("nvs", [DEPTH, 128, DSGU])

    with contextlib.ExitStack() as st:
        k = K(nc, st)
        out_toks = []

        WS = [512, 128]
        x = [[k.sb(f"x{s}_{c}", [128, WS[s]], F32) for c in range(NCH)] for s in range(2)]
        xn = [[k.sb(f"xn{s}_{c}", [128, WS[s]], BF16) for c in range(NCH)] for s in range(2)]
        ub = [[k.sb(f"u{s}_{c}", [128, WS[s]], BF16) for c in range(4)] for s in range(2)]
        mix = [[k.sb(f"mix{s}_{c}", [128, WS[s]], BF16) for c in range(12)] for s in range(2)]
        hid = [[k.sb(f"hid{s}_{j}", [128, WS[s]], BF16) for j in range(8)] for s in range(2)]
        xbuf = Rot([k.sb(f"xbuf{i}", [128, 516], F32) for i in range(2)])
        xcb_ = Rot([k.sb(f"xc{i}", [128, 512], F32) for i in range(4)])
        xcbf = Rot([k.sb(f"xcb{i}", [128, 512], BF16) for i in range(4)])
        Rb = Rot([k.sb(f"R{i}", [128, 512], F32) for i in range(2)])
        Ab = Rot([k.sb(f"A{i}", [128, 512], F32) for i in range(3)])
        Ib = Rot([k.sb(f"I{i}", [128, 512], F32) for i in range(3)])
        Mb = Rot([k.sb(f"M{i}", [128, 512], F32) for i in range(2)])
        hb = Rot([k.sb(f"h{i}", [128, 512], F32) for i in range(2)])
        gv = [k.sb(f"gv{i}", [128, DSGU], F32) for i in range(5)]
        vnb = [k.sb(f"vnb{i}", [128, DSGU], BF16) for i in range(5)]
        vo = k.sb("vo", [128, DSGU], F32)
        stats = k.sb("stats", [128, 5, 6], F32)
        mv = k.sb("mv", [128, 5, 2], F32)
        lnr = k.sb("lnr", [128, 5], F32)
        sqb = Rot([k.sb(f"sqb{i}", [128, 512], BF16) for i in range(3)])
        rstd = Rot([k.sb(f"rstd{i}", [128, 512], F32) for i in range(2)])
        banks = Rot([k.ps(f"bank{i}", [128, 512], F32) for i in range(8)])
        ring = [k.sb(f"ring{i}", [128, UNIT], BF16) for i in range(NSLOT)]
        ring_src = [k.dma_src(f"ring{i}") for i in range(NSLOT)]

        vecs = k.sb("vecs", [128, NV], F32)
        c8 = k.sb("c8", [128, 16], F32)
        hc8 = k.sb("hc8", [128, 16], F32)
        hrb = k.sb("hrb", [128, 16], F32)
        hib = k.sb("hib", [128, 16], F32)
        lnp = [k.sb(f"lnp{i}", [128, DSGU], F32) for i in range(2)]
        lnp_src = k.dma_src("lnp")
        swT = [[k.sb(f"swT{l}{i}", [128, 8, 128], BF16) for i in range(2)] for l in range(DEPTH)]
        gbd = [[k.sb(f"gbd{l}{i}", [128, NCH, 128], BF16) for i in range(2)] for l in range(DEPTH)]
        brow = [[k.sb(f"brow{l}{i}", [40, 128], F32) for i in range(2)] for l in range(DEPTH)]
        bhl = [[k.sb(f"bhl{l}{i}", [40, 128], BF16) for i in range(2)] for l in range(DEPTH)]
        bhi_t = k.sb("bhi_t", [40, 128], BF16)
        btmp = k.sb("btmp", [40, 128], F32)
        sel = k.sb("sel", [40, 8, 64], BF16)
        onesb = k.sb("onesb", [128, 128], BF16)
        onesb_w = k.sb("onesb_w", [128, 512], BF16)
        hstate = [k.sb(f"hstate{l}", [128, NCH], F32) for l in range(DEPTH)]
        hist = [k.sb(f"hist{l}", [128, NCH, 3], F32) for l in range(DEPTH)]
        h0s = [k.sb(f"h0s{l}", [128, NCH, 16], F32) for l in range(DEPTH)]
        cv0s = [k.sb(f"cv0s{l}", [128, NCH, 16, 3], F32) for l in range(DEPTH)]
        hs_stage = [k.sb(f"hsst{l}", [128, NCH, 16], F32) for l in range(DEPTH)]
        cs_stage = [k.sb(f"csst{l}", [128, NCH, 16, 3], F32) for l in range(DEPTH)]
        tmp16 = k.sb("tmp16", [128, 16], F32)

        csrc = k.dma_src("const")
        csrc_sw = k.dma_src("const_sw")
        xsrc = [[k.dma_src(f"xin{s}_{c}") for c in range(NCH)] for s in range(2)]
        osrc = k.dma_src("out")
        ysrc = [k.dma_src(f"yout{c}") for c in range(NCH)]
        vo.dsrc = k.dma_src("vo")

        def unit_list(l):
            us = []
            for cb in (2, 4, 1, 0, 3, 5):
                us.append(("in", l, cb))
            for ob in range(4):
                us.append(("out", l, ob))
            for q in range(4):
                us += [("w1", l, 2 * q), ("w1", l, 2 * q + 1), ("w2", l, 2 * q), ("w2", l, 2 * q + 1)]
            return us

        seq = []
        for _g in groups:
            for l in range(DEPTH):
                seq += unit_list(l)
        st_ = {"next": 0, "free": list(range(NSLOT)), "live": {}}

        def unit_view(key, slot):
            t = ring[slot]
            kind = key[0]
            if kind in ("in", "w1"):
                return t[:, 0:8 * 512].rearrange("p (k n) -> p k n", n=512)
            if kind == "out":
                return t[:, 0:12 * 256].rearrange("p (k n) -> p k n", n=256)
            return t[:, 0:4 * 1024].rearrange("p (k n) -> p k n", n=1024)

        def unit_src(key):
            kind, l, i = key
            if kind == "in":
                return w_in_d[l].rearrange("(k p) n -> p k n", p=128)[:, :, i * 512:(i + 1) * 512]
            if kind == "out":
                return w_out_d[l].rearrange("(k p) n -> p k n", p=128)[:, :, i * 256:(i + 1) * 256]
            if kind == "w1":
                return w1_d[l].rearrange("(k p) n -> p k n", p=128)[:, :, i * 512:(i + 1) * 512]
            return w2_d[l, i * 512:(i + 1) * 512, :].rearrange("(k p) n -> p k n", p=128)

        def prefetch():
            while st_["next"] < len(seq) and st_["free"]:
                idx = st_["next"]
                key = seq[idx]
                slot = st_["free"].pop(0)
                st_["next"] += 1
                view = unit_view(key, slot)
                k.dma("pool", ring_src[slot], lambda h, v=view, s=unit_src(key): h.dma_start(out=v, in_=s),
                      writes=[ring[slot]])
                st_["live"][idx] = slot

        pos = {"i": 0}

        def get_unit(key):
            idx = pos["i"]
            assert seq[idx] == key, (seq[idx], key)
            pos["i"] += 1
            prefetch()
            assert idx in st_["live"], "ring too small for access pattern"
            slot = st_["live"][idx]
            return idx, ring[slot], unit_view(key, slot)

        def release(idx):
            slot = st_["live"].pop(idx)
            st_["free"].append(slot)
            prefetch()

        k.op("pool", lambda h: h.memset(onesb[:], 1.0 / D), writes=[onesb])
        k.op("pool", lambda h: h.memset(onesb_w[:], 0.5), writes=[onesb_w])
        kind0, tok00, tw0 = TILES[groups[0][0]]
        for c_ in range(NCH):
            k.dma("sp", xsrc[0][c_], lambda h, c_=c_: h.dma_start(out=x[0][c_][:, :tw0], in_=xp_d[:, c_, tok00:tok00 + tw0]),
                  writes=[x[0][c_]])
        const_tiles = []

        def ld(q, src, dst_t, dst_ap, src_ap):
            const_tiles.append((dst_t, src))
            return k.dma(q, src, lambda h: h.dma_start(out=dst_ap, in_=src_ap), writes=[dst_t])

        vsrc = k.dma_src("vecs")
        k.dma("sp", vsrc, lambda h: h.dma_start(out=vecs[:], in_=vecs_d), writes=[vecs])
        ld("pool", csrc_sw, sel, sel[:], sel_d)
        for l in range(DEPTH):
            for i in range(2):
                ld("sp", csrc, brow[l][i], brow[l][i][:], brow_d[l, i])
                ld("pool", csrc_sw, swT[l][i], swT[l][i][:], sw_d[l, i])
                ld("pool", csrc_sw, gbd[l][i], gbd[l][i][:], gbd_d[l, i])
            ld("sp", csrc, h0s[l], h0s[l][:], h0s_d[l])
            ld("sp", csrc, cv0s[l], cv0s[l][:], cv0s_d[l])
        for t_, src_ in const_tiles:
            t_.w = (src_, src_.count)
        def mask_prep():
            for l in range(DEPTH):
                for i in range(2):
                    k.op("pool", lambda h, t=swT[l][i]: h.affine_select(
                        out=t[:], in_=t[:], pattern=[[0, 8], [1, 128]], compare_op=ALU.is_ge, fill=0.0,
                        base=0, channel_multiplier=-1), reads=[swT[l][i]], writes=[swT[l][i]])

        def const_prep():
            for l in range(DEPTH):
                k.op("dve", lambda h, l=l: h.memset(hstate[l][:], 0.0), writes=[hstate[l]])
                k.op("dve", lambda h, l=l: h.memset(hist[l][:], 0.0), writes=[hist[l]])
                for i in range(2):
                    k.op("dve", lambda h, l=l, i=i: h.tensor_copy(out=bhi_t[:], in_=brow[l][i][:]),
                         reads=[brow[l][i]], writes=[bhi_t])
                    k.op("dve", lambda h: h.tensor_copy(out=btmp[:], in_=bhi_t[:]), reads=[bhi_t], writes=[btmp])
                    k.op("dve", lambda h, l=l, i=i: h.tensor_tensor(out=btmp[:], in0=brow[l][i][:], in1=btmp[:], op=ALU.subtract),
                         reads=[brow[l][i], btmp], writes=[btmp])
                    k.op("dve", lambda h, l=l, i=i: h.memset(bhl[l][i][:], 0.0), writes=[bhl[l][i]])
                    k.op("dve", lambda h, l=l, i=i: h.tensor_copy(out=bhl[l][i][0:8, :], in_=bhi_t[0:8, :]),
                         reads=[bhi_t], writes=[bhl[l][i]])
                    k.op("dve", lambda h, l=l, i=i: h.tensor_copy(out=bhl[l][i][32:40, :], in_=btmp[32:40, :]),
                         reads=[btmp], writes=[bhl[l][i]])
            for l in range(DEPTH):
                lam = vecs[:, l * LV + V_LAM:l * LV + V_LAM + 8]
                o = c8[:, l * 8:(l + 1) * 8]
                k.op("act", lambda h, lam=lam, o=o: h.activation(out=o, in_=lam, func=AF.Exp, scale=-1.0),
                     reads=[vecs], writes=[c8])
                k.op("act", lambda h, o=o: h.activation(out=o, in_=o, func=AF.Ln, bias=1.0), reads=[c8], writes=[c8])
                k.op("dve", lambda h, l=l, o=o: h.tensor_scalar(out=hc8[:, l * 8:(l + 1) * 8], in0=o, scalar1=-4.0,
                                                               scalar2=None, op0=ALU.mult), reads=[c8], writes=[hc8])
                k.op("dve", lambda h, o=o: h.tensor_scalar(out=o, in0=o, scalar1=-8.0, scalar2=None, op0=ALU.mult),
                     reads=[c8], writes=[c8])
                k.op("dve", lambda h, l=l: h.tensor_scalar(out=hrb[:, l * 8:(l + 1) * 8],
                                                          in0=vecs[:, l * LV + V_RB:l * LV + V_RB + 8],
                                                          scalar1=0.5, scalar2=None, op0=ALU.mult),
                     reads=[vecs], writes=[hrb])
                k.op("dve", lambda h, l=l: h.tensor_scalar(out=hib[:, l * 8:(l + 1) * 8],
                                                          in0=vecs[:, l * LV + V_IB:l * LV + V_IB + 8],
                                                          scalar1=0.5, scalar2=None, op0=ALU.mult),
                     reads=[vecs], writes=[hib])

        def mm_group(out_ap, pairs, reads, bank, split_reads=None):
            if split_reads is not None:
                n = len(pairs)
                tok = None
                for i, (lt, rh) in enumerate(pairs):
                    tok = k.op("pe", lambda h, lt=lt, rh=rh, i=i: h.matmul(out_ap, lhsT=lt, rhs=rh, start=(i == 0),
                                                                          stop=(i == n - 1)),
                               reads=split_reads[i], writes=[bank])
                return tok

            def fn(h):
                ins = None
                n = len(pairs)
                for i, (lt, rh) in enumerate(pairs):
                    ins = h.matmul(out_ap, lhsT=lt, rhs=rh, start=(i == 0), stop=(i == n - 1))
                return ins
            return k.op("pe", fn, reads=reads, writes=[bank])

        fresh = {0: 0, 1: 0}

        def mm_xn(out_ap, wview, wtile, col0, s, tw, bank):
            pairs = [(wview[:, kk, col0:col0 + 128], xn[s][kk][:, :tw]) for kk in range(NCH)]
            if fresh[s] > 0:
                fresh[s] -= 1
                return mm_group(out_ap, pairs, None, bank, split_reads=[[wtile, xn[s][kk]] for kk in range(NCH)])
            return mm_group(out_ap, pairs, [wtile] + xn[s], bank)

        def rms_norm(s, tw, gcol, dst_kind, tile_info=None, xt=None):
            xt = x[s] if xt is None else xt
            if dst_kind == "xn":
                fresh[s] = 2
            bank = banks.next()
            for c in range(NCH):
                sq = sqb.next()
                k.op("act", lambda h, c=c, sq=sq: h.activation(out=sq[:, :tw], in_=xt[c][:, :tw], func=AF.Square),
                     reads=[xt[c]], writes=[sq])
                k.op("pe", lambda h, c=c, sq=sq: h.matmul(bank[:, :tw], lhsT=onesb[:], rhs=sq[:, :tw],
                                                          start=(c == 0), stop=(c == NCH - 1)),
                     reads=[onesb, sq], writes=[bank])
            if dst_kind == "xn" and tw == 512 and N_WARM > 0:
                wb = banks.next()
                k.op("pe", lambda h: [h.matmul(wb[:, :tw], lhsT=onesb[:], rhs=onesb_w[:, :tw], start=True, stop=True)
                                      for _ in range(N_WARM)][-1], reads=[onesb, onesb_w], writes=[wb])
            rs = rstd.next()
            k.op("act", lambda h: h.activation(out=rs[:, :tw], in_=bank[:, :tw], func=AF.Ln, bias=EPS),
                 reads=[bank], writes=[rs])
            k.op("act", lambda h: h.activation(out=rs[:, :tw], in_=rs[:, :tw], func=AF.Exp, scale=-0.5),
                 reads=[rs], writes=[rs])
            for c in range(NCH):
                g = vecs[:, gcol + c:gcol + c + 1]
                if dst_kind == "xn":
                    k.op("dve", lambda h, c=c, g=g: h.scalar_tensor_tensor(
                        out=xn[s][c][:, :tw], in0=xt[c][:, :tw], scalar=g, in1=rs[:, :tw],
                        op0=ALU.mult, op1=ALU.mult), reads=[xt[c], vecs, rs], writes=[xn[s][c]])
                else:
                    k.op("dve", lambda h, c=c, g=g: h.scalar_tensor_tensor(
                        out=xt[c][:, :tw], in0=xt[c][:, :tw], scalar=g, in1=rs[:, :tw],
                        op0=ALU.mult, op1=ALU.mult), reads=[xt[c], vecs, rs], writes=[xt[c]])
                    kind, tok0, _ = tile_info
                    dst = yp_d[:, c, tok0:tok0 + tw] if kind == "p" else ys_d[:, c, :]
                    ysrc_c = ysrc[c] if kind == "p" else osrc
                    out_toks.append(k.dma("sp", ysrc_c, lambda h, dst=dst, c=c: h.dma_start(out=dst, in_=xt[c][:, :tw]),
                                          reads=[xt[c]]))

        def v3(ap, tw):
            return ap.rearrange("p (b t) -> p b t", t=8)

        prenormed = False
        for gi, g in enumerate(groups):
            tl = [(s, TILES[ti]) for s, ti in enumerate(g)]
            nxt = TILES[groups[gi + 1][0]] if gi + 1 < len(groups) else None
            for s, (kind, tok0, tw) in tl:
                if (s == 0 and (prenormed or gi == 0)) or (s == 1 and prenormed):
                    continue
                src_ap = xp_d[:, :, tok0:tok0 + tw] if kind == "p" else xs_d
                for c_ in range(NCH):
                    k.dma("sp", xsrc[s][c_], lambda h, s=s, tw=tw, a=src_ap, c_=c_: h.dma_start(out=x[s][c_][:, :tw], in_=a[:, c_, :]),
                          writes=[x[s][c_]])
            for l in range(DEPTH):
                vb = l * LV
                k.dma("sp", lnp_src, lambda h: [h.dma_start(out=lnp[i_][:], in_=lnp_d[l, i_]) for i_ in range(2)],
                      writes=[lnp[0], lnp[1]])
                for s, (kind, tok0, tw) in tl:
                    if l == 0 and prenormed:
                        continue
                    rms_norm(s, tw, vb + V_NMIX, "xn")
                if gi == 0 and l == 0:
                    const_prep()
                lst = {}
                uxs = {}
                uys = {}
                pre_xb = {}
                pre_yb = {}
                pending_casts = []

                def flush_casts():
                    while pending_casts:
                        xcf, xc, tw = pending_casts.pop(0)
                        k.op("act", lambda h, xcf=xcf, xc=xc, tw=tw: h.copy(out=xcf[:, :tw], in_=xc[:, :tw]),
                             reads=[xc], writes=[xcf])

                def stage_a(items):
                    st_a = []
                    for (c, s, kind, tok0, tw) in items:
                        half, cc = divmod(c, 4)
                        if half not in uxs:
                            uxs[half] = get_unit(("in", l, 2 + half))
                        ux = uxs[half]
                        if (c, s) in pre_xb:
                            bk = pre_xb.pop((c, s))
                        else:
                            bk = banks.next()
                            mm_xn(bk[:, :tw], ux[2], ux[1], cc * 128, s, tw, bk)
                        xb = xbuf.next()
                        xc = xcb_.next()
                        if kind == "p":
                            k.op("act", lambda h, xb=xb, bk=bk, tw=tw: h.copy(out=xb[:, 3:3 + tw], in_=bk[:, :tw]),
                                 reads=[bk], writes=[xb])
                            xin = [xb[:, kk:kk + tw] for kk in range(4)]
                            xco = xc[:, :tw]
                            xbv = None
                        else:
                            xbv = xb[:, 0:176].rearrange("p (b q) -> p b q", q=11)
                            k.op("act", lambda h, xbv=xbv, bk=bk, tw=tw: h.copy(out=xbv[:, :, 3:11], in_=v3(bk[:, :tw], tw)),
                                 reads=[bk], writes=[xb])
                            xin = [xbv[:, :, kk:kk + 8] for kk in range(4)]
                            xco = v3(xc[:, :tw], tw)
                        st_a.append((c, s, kind, tw, xb, xc, xbv, xin, xco))
                    for (c, s, kind, tw, xb, xc, xbv, xin, xco) in st_a:
                        if kind == "p":
                            k.op("act", lambda h, xb=xb, c=c: h.copy(out=xb[:, 0:3], in_=hist[l][:, c, :]),
                                 reads=[hist[l]], writes=[xb])
                        else:
                            k.op("act", lambda h, xbv=xbv, c=c: h.copy(out=xbv[:, :, 0:3], in_=cv0s[l][:, c, :, :]),
                                 reads=[cv0s[l]], writes=[xb])
                    for kk in (3, 0, 1, 2):
                        for (c, s, kind, tw, xb, xc, xbv, xin, xco) in st_a:
                            cwk = vecs[:, vb + V_CW + 8 * kk + c:vb + V_CW + 8 * kk + c + 1]
                            if kk == 3:
                                cbias = vecs[:, vb + V_CB + c:vb + V_CB + c + 1]
                                k.op("dve", lambda h, xin=xin, xco=xco, cwk=cwk, cbias=cbias: h.tensor_scalar(
                                    out=xco, in0=xin[3], scalar1=cwk, scalar2=cbias, op0=ALU.mult, op1=ALU.add),
                                    reads=[xb, vecs], writes=[xc])
                            else:
                                k.op("dve", lambda h, xin=xin, xco=xco, cwk=cwk, kk=kk: h.scalar_tensor_tensor(
                                    out=xco, in0=xin[kk], scalar=cwk, in1=xco, op0=ALU.mult, op1=ALU.add),
                                    reads=[xb, vecs, xc], writes=[xc])
                    for (c, s, kind, tw, xb, xc, xbv, xin, xco) in st_a:
                        if kind == "p":
                            k.op("act", lambda h, xb=xb, c=c, tw=tw: h.copy(out=hist[l][:, c, :], in_=xb[:, tw:tw + 3]),
                                 reads=[xb], writes=[hist[l]])
                        else:
                            k.op("act", lambda h, xbv=xbv, c=c: h.copy(out=cs_stage[l][:, c, :, :], in_=xbv[:, :, 8:11]),
                                 reads=[xb], writes=[cs_stage[l]])
                        xcf = xcbf.next()
                        pending_casts.append((xcf, xc, tw))
                        lst[(c, s)] = {"xc": xc, "xcf": xcf}
                        if c % 4 == 3 and s == len(tl) - 1:
                            release(uxs[c // 4][0])

                def b_pe(items):
                    sb_ = []
                    for (c, s, kind, tok0, tw) in items:
                        d = lst[(c, s)]
                        xc, xcf = d["xc"], d["xcf"]
                        bR = banks.next()
                        mm_group(bR[:, :tw], [(gbd[l][0][:, c, :], xcf[:, :tw])], [gbd[l][0], xcf], bR)
                        bI = banks.next()
                        mm_group(bI[:, :tw], [(gbd[l][1][:, c, :], xcf[:, :tw])], [gbd[l][1], xcf], bI)
                        sb_.append(dict(c=c, s=s, kind=kind, tw=tw, xc=xc, bR=bR, bI=bI, R=Rb.next(), A=Ab.next(),
                                        I=Ib.next(), M=Mb.next(), hh=hb.next(), col=l * 8 + c))
                    return sb_

                def b_rest(sb_):
                    for e in sb_:
                        k.op("act", lambda h, e=e: h.activation(
                            out=e["R"][:, :e["tw"]], in_=e["bR"][:, :e["tw"]], func=AF.Tanh,
                            bias=hrb[:, e["col"]:e["col"] + 1], scale=0.5), reads=[e["bR"], hrb], writes=[e["R"]])
                    for e in sb_:
                        k.op("act", lambda h, e=e: h.activation(
                            out=e["I"][:, :e["tw"]], in_=e["bI"][:, :e["tw"]], func=AF.Tanh,
                            bias=hib[:, e["col"]:e["col"] + 1], scale=0.5), reads=[e["bI"], hib], writes=[e["I"]])
                    for e in sb_:
                        k.op("act", lambda h, e=e: h.activation(
                            out=e["M"][:, :e["tw"]], in_=e["R"][:, :e["tw"]], func=AF.Exp,
                            bias=c8[:, e["col"]:e["col"] + 1], scale=c8[:, e["col"]:e["col"] + 1]),
                            reads=[e["R"], c8], writes=[e["M"]])
                    for e in sb_:
                        k.op("act", lambda h, e=e: h.activation(
                            out=e["A"][:, :e["tw"]], in_=e["R"][:, :e["tw"]], func=AF.Exp,
                            bias=hc8[:, e["col"]:e["col"] + 1], scale=hc8[:, e["col"]:e["col"] + 1]),
                            reads=[e["R"], hc8], writes=[e["A"]])
                    for e in sb_:
                        k.op("act", lambda h, e=e: h.activation(out=e["M"][:, :e["tw"]], in_=e["M"][:, :e["tw"]],
                                                               func=AF.Ln, bias=1.0, scale=-1.0),
                             reads=[e["M"]], writes=[e["M"]])
                    for e in sb_:
                        k.op("dve", lambda h, e=e: h.scalar_tensor_tensor(
                            out=e["I"][:, :e["tw"]], in0=e["I"][:, :e["tw"]], scalar=1.0, in1=e["xc"][:, :e["tw"]],
                            op0=ALU.add, op1=ALU.mult), reads=[e["I"], e["xc"]], writes=[e["I"]])
                    for e in sb_:
                        k.op("act", lambda h, e=e: h.activation(out=e["M"][:, :e["tw"]], in_=e["M"][:, :e["tw"]],
                                                               func=AF.Exp, scale=0.5),
                             reads=[e["M"]], writes=[e["M"]])
                    for e in sb_:
                        k.op("dve", lambda h, e=e: h.scalar_tensor_tensor(
                            out=e["M"][:, :e["tw"]], in0=e["I"][:, :e["tw"]], scalar=0.5, in1=e["M"][:, :e["tw"]],
                            op0=ALU.mult, op1=ALU.mult), reads=[e["I"], e["M"]], writes=[e["M"]])
                    for e in sb_:
                        tw, A, M, hh, c = e["tw"], e["A"], e["M"], e["hh"], e["c"]
                        if e["kind"] == "s":
                            Av = v3(A[:, :tw], tw); Mv = v3(M[:, :tw], tw)
                            k.op("dve", lambda h, Av=Av, c=c: h.tensor_tensor(
                                out=tmp16[:], in0=Av[:, :, 0], in1=h0s[l][:, c, :], op=ALU.mult),
                                reads=[A, h0s[l]], writes=[tmp16])
                            k.op("dve", lambda h, Mv=Mv: h.tensor_tensor(
                                out=Mv[:, :, 0], in0=Mv[:, :, 0], in1=tmp16[:], op=ALU.add),
                                reads=[M, tmp16], writes=[M])
                            k.op("dve", lambda h, Av=Av: h.memset(Av[:, :, 0], 0.0), reads=[A], writes=[A])
                    for e in sb_:
                        tw, A, M, hh, c = e["tw"], e["A"], e["M"], e["hh"], e["c"]
                        if e["kind"] == "p":
                            k.op("dve", lambda h, hh=hh, A=A, M=M, c=c, tw=tw: h.tensor_tensor_scan(
                                out=hh[:, :tw], data0=A[:, :tw], data1=M[:, :tw], initial=hstate[l][:, c:c + 1],
                                op0=ALU.mult, op1=ALU.add), reads=[A, M, hstate[l]], writes=[hh])
                        else:
                            k.op("dve", lambda h, hh=hh, A=A, M=M, tw=tw: h.tensor_tensor_scan(
                                out=hh[:, :tw], data0=A[:, :tw], data1=M[:, :tw], initial=0.0,
                                op0=ALU.mult, op1=ALU.add), reads=[A, M], writes=[hh])
                    for e in sb_:
                        tw, hh, c, s = e["tw"], e["hh"], e["c"], e["s"]
                        if e["kind"] == "p":
                            k.op("dve", lambda h, hh=hh, c=c, tw=tw: h.tensor_copy(out=hstate[l][:, c:c + 1], in_=hh[:, tw - 1:tw]),
                                 reads=[hh], writes=[hstate[l]])
                        else:
                            k.op("dve", lambda h, hh=hh, c=c, tw=tw: h.tensor_copy(
                                out=hs_stage[l][:, c, :], in_=v3(hh[:, :tw], tw)[:, :, 7]),
                                reads=[hh], writes=[hs_stage[l]])
                    for e in sb_:
                        tw, hh, c, s = e["tw"], e["hh"], e["c"], e["s"]
                        k.op("dve", lambda h, hh=hh, c=c, s=s, tw=tw: h.tensor_tensor(
                            out=mix[s][4 + c][:, :tw], in0=hh[:, :tw], in1=mix[s][4 + c][:, :tw], op=ALU.mult),
                            reads=[hh, mix[s][4 + c]], writes=[mix[s][4 + c]])

                def yb_task(c):
                    half, cc = divmod(c, 4)
                    if half not in uys:
                        uys[half] = get_unit(("in", l, 4 + half))
                    uy = uys[half]
                    for s, (kind, tok0, tw) in tl:
                        if (c, s) in pre_yb:
                            bY = pre_yb.pop((c, s))
                        else:
                            bY = banks.next()
                            mm_group(bY[:, :tw], [(uy[2][:, kk, cc * 128:(cc + 1) * 128], xn[s][kk][:, :tw]) for kk in range(NCH)],
                                     [uy[1]] + xn[s], bY)
                        k.op("act", lambda h, bY=bY, s=s, c=c, tw=tw: h.activation(
                            out=mix[s][4 + c][:, :tw], in_=bY[:, :tw], func=AF.Gelu_apprx_tanh),
                            reads=[bY], writes=[mix[s][4 + c]])
                    if cc == 3:
                        release(uy[0])

                subs = []
                for s, (kind, tok0, tw) in tl:
                    for j in range(tw // 128):
                        subs.append((s, kind, j))
                ns = len(subs)
                uvu = {}

                def v_task(n):
                    s, kind, j = subs[n]
                    uv = uvu["v"]
                    bk = banks.next()
                    mm_group(bk[:, :], [(xn[s][kk][:, j * 128:(j + 1) * 128], uv[2][:, kk, :]) for kk in range(NCH)],
                             [uv[1]] + xn[s], bk)
                    k.op("act", lambda h, n=n, bk=bk: h.activation(out=gv[n][:], in_=bk[:], func=AF.Gelu_apprx_tanh),
                         reads=[bk], writes=[gv[n]])
                    k.op("dve", lambda h, n=n: h.bn_stats(out=stats[:, n, :], in_=gv[n][:]), reads=[gv[n]], writes=[stats])
                    k.op("dve", lambda h, n=n: h.bn_aggr(out=mv[:, n, :], in_=stats[:, n, :]), reads=[stats], writes=[mv])
                    lnst["v_left"] -= 1
                    if n == ns - 1:
                        release(uv[0])

                def u_task(m, s, tw, last):
                    uu = uvu["u"]
                    bk = banks.next()
                    mm_group(bk[:, :tw], [(uu[2][:, kk, m * 128:(m + 1) * 128], xn[s][kk][:, :tw]) for kk in range(NCH)],
                             [uu[1]] + xn[s], bk)
                    k.op("act", lambda h, bk=bk, s=s, m=m, tw=tw: h.activation(
                        out=ub[s][m][:, :tw], in_=bk[:, :tw], func=AF.Gelu_apprx_tanh), reads=[bk], writes=[ub[s][m]])
                    if last:
                        release(uu[0])

                lnst = {"act": False, "dve": False, "v_left": ns}

                def ln_act():
                    k.op("act", lambda h: h.activation(out=lnr[:, :ns], in_=mv[:, :ns, 1], func=AF.Ln, bias=EPS),
                         reads=[mv], writes=[lnr])
                    k.op("act", lambda h: h.activation(out=lnr[:, :ns], in_=lnr[:, :ns], func=AF.Exp, scale=-0.5),
                         reads=[lnr], writes=[lnr])

                def ln_dve():
                    for n, (s, kind, j) in enumerate(subs):
                        k.op("dve", lambda h, n=n: h.scalar_tensor_tensor(
                            out=gv[n][:], in0=gv[n][:], scalar=mv[:, n, 0:1], in1=lnp[0][:],
                            op0=ALU.subtract, op1=ALU.mult), reads=[gv[n], mv, lnp[0]], writes=[gv[n]])
                        if kind == "p":
                            k.op("dve", lambda h, n=n: h.scalar_tensor_tensor(
                                out=vnb[n][:], in0=gv[n][:], scalar=lnr[:, n:n + 1], in1=lnp[1][:],
                                op0=ALU.mult, op1=ALU.add), reads=[gv[n], lnr, lnp[1]], writes=[vnb[n]])
                        else:
                            k.op("dve", lambda h, n=n: h.scalar_tensor_tensor(
                                out=vo[:], in0=gv[n][:], scalar=lnr[:, n:n + 1], in1=lnp[1][:],
                                op0=ALU.mult, op1=ALU.add), reads=[gv[n], lnr, lnp[1]], writes=[vo])
                            k.op("act", lambda h, n=n: h.copy(out=vnb[n][:], in_=vo[:]), reads=[vo], writes=[vnb[n]])
                            out_toks.append(k.dma("sp", vo.dsrc, lambda h: h.dma_start(out=nvs_d[l], in_=vo[:]), reads=[vo]))
                    lnst["dve"] = True

                def sg_task(hp, s, kind, tw):
                    assert lnst["dve"]
                    ki = 0 if kind == "p" else 1
                    bk = banks.next()
                    mysubs = [(n, j) for n, (s2, _k, j) in enumerate(subs) if s2 == s]

                    def fn(h, bk=bk, mysubs=mysubs, ki=ki, hp=hp):
                        ins = None
                        for n, j in mysubs:
                            cols = slice(j * 128, (j + 1) * 128)
                            for hh_ in range(2):
                                hd = 2 * hp + hh_
                                o = bk[64 * hh_:64 * hh_ + 64, cols]
                                h.matmul(o, lhsT=vnb[n][:, hd * 64:(hd + 1) * 64], rhs=swT[l][ki][:, hd, :],
                                         start=True, stop=False)
                                ins = h.matmul(o, lhsT=sel[:, hd, :], rhs=bhl[l][ki][:], start=False, stop=True)
                        return ins
                    k.op("pe", fn, reads=[vnb[n] for n, _ in mysubs] + [swT[l][ki], sel, bhl[l][ki]],
                         writes=[bk])
                    k.op("dve", lambda h, bk=bk, s=s, hp=hp, tw=tw: h.tensor_tensor(
                        out=mix[s][hp][:, :tw], in0=ub[s][hp][:, :tw], in1=bk[:, :tw], op=ALU.mult),
                        reads=[ub[s][hp], bk], writes=[mix[s][hp]])

                extras = [(lambda n=n: v_task(n)) for n in range(ns)]
                ul = [(m, s, tw) for m in range(4) for s, (kind, tok0, tw) in tl]
                for i_, (m, s, tw) in enumerate(ul):
                    extras.append(lambda m=m, s=s, tw=tw, last=(i_ == len(ul) - 1): u_task(m, s, tw, last))
                for hp in range(4):
                    for s, (kind, tok0, tw) in tl:
                        extras.append(lambda hp=hp, s=s, kind=kind, tw=tw: sg_task(hp, s, kind, tw))

                if len(tl) == 1:
                    s0, (kind0, tok00, tw0) = tl[0]
                    pairs = [[(2 * i, s0, kind0, tok00, tw0), (2 * i + 1, s0, kind0, tok00, tw0)] for i in range(4)]
                else:
                    pairs = [[(c, s, kind, tok0, tw) for s, (kind, tok0, tw) in tl] for c in range(NCH)]
                per_it = -(-(len(extras) + 1) // len(pairs))
                sched = [4, 4, 4, 1] if len(tl) == 1 else [per_it] * len(pairs)
                uxs[0] = get_unit(("in", l, 2))
                if gi == 0 and l == 0:
                    mask_prep()
                grp0 = []
                if len(tl) == 1:
                    for (c, s, kind, tok0, tw) in pairs[0]:
                        bk = banks.next()
                        pre_xb[(c, s)] = bk
                        grp0.append((uxs[0], c, s, tw, bk))
                    fresh[tl[0][0]] = 0
                for kk in range(NCH):
                    for (un, c, s, tw, bk) in grp0:
                        k.op("pe", lambda h, un=un, c=c, s=s, tw=tw, bk=bk, kk=kk: h.matmul(
                            bk[:, :tw], lhsT=un[2][:, kk, (c % 4) * 128:(c % 4 + 1) * 128], rhs=xn[s][kk][:, :tw],
                            start=(kk == 0), stop=(kk == NCH - 1)), reads=[un[1], xn[s][kk]], writes=[bk])
                stage_a(pairs[0])
                flush_casts()
                for c in sorted(set(it[0] for it in pairs[0])):
                    yb_task(c)
                for i, items in enumerate(pairs):
                    if i + 1 < len(pairs):
                        stage_a(pairs[i + 1])
                    sb_ = b_pe(items)
                    b_rest(sb_)
                    if lnst["v_left"] == 0 and not lnst["act"]:
                        ln_act()
                        lnst["act"] = True
                        extras.insert(0, ln_dve)
                    flush_casts()
                    if i + 1 < len(pairs):
                        for c in sorted(set(it[0] for it in pairs[i + 1])):
                            yb_task(c)
                    if i == 0:
                        uvu["v"] = get_unit(("in", l, 1))
                        uvu["u"] = get_unit(("in", l, 0))
                    for _ in range(sched[i]):
                        if extras:
                            extras.pop(0)()
                assert lnst["act"]
                while extras:
                    extras.pop(0)()
                early_k = [4, 5, 6, 7, 8, 9]
                late_k = [0, 1, 2, 3, 10, 11]
                uo0 = get_unit(("out", l, 0))
                uo1 = get_unit(("out", l, 1))
                first = []
                for m in range(4):
                    uo = uo0 if m < 2 else uo1
                    for s, (kind, tok0, tw) in tl:
                        first.append((m, s, tw, uo, banks.next()))
                for phase, ks in enumerate((early_k, late_k)):
                    for (m, s, tw, uo, bk) in first:
                        for i_, kk in enumerate(ks):
                            k.op("pe", lambda h, bk=bk, uo=uo, m=m, s=s, tw=tw, kk=kk, st=(phase == 0 and i_ == 0),
                                 sp_=(phase == 1 and i_ == len(ks) - 1): h.matmul(
                                     bk[:, :tw], lhsT=uo[2][:, kk, (m % 2) * 128:(m % 2 + 1) * 128], rhs=mix[s][kk][:, :tw],
                                     start=st, stop=sp_), reads=[uo[1], mix[s][kk]], writes=[bk])
                for (m, s, tw, uo, bk) in first:
                    k.op("dve", lambda h, bk=bk, s=s, m=m, tw=tw: h.tensor_tensor(
                        out=x[s][m][:, :tw], in0=x[s][m][:, :tw], in1=bk[:, :tw], op=ALU.add),
                        reads=[x[s][m], bk], writes=[x[s][m]])
                release(uo0[0])
                release(uo1[0])
                for ob in range(2, 4):
                    uo = get_unit(("out", l, ob))
                    for mm_ in range(2):
                        m = ob * 2 + mm_
                        for s, (kind, tok0, tw) in tl:
                            bk = banks.next()
                            mm_group(bk[:, :tw], [(uo[2][:, kk, mm_ * 128:(mm_ + 1) * 128], mix[s][kk][:, :tw]) for kk in range(12)],
                                     [uo[1]] + mix[s], bk)
                            k.op("dve", lambda h, bk=bk, s=s, m=m, tw=tw: h.tensor_tensor(
                                out=x[s][m][:, :tw], in0=x[s][m][:, :tw], in1=bk[:, :tw], op=ALU.add),
                                reads=[x[s][m], bk], writes=[x[s][m]])
                    release(uo[0])
                for s, (kind, tok0, tw) in tl:
                    if kind == "s":
                        out_toks.append(k.dma("sp", osrc, lambda h: h.dma_start(out=nhs_d[l], in_=hs_stage[l][:]),
                                              reads=[hs_stage[l]]))
                        out_toks.append(k.dma("sp", osrc, lambda h: h.dma_start(out=ncs_d[l], in_=cs_stage[l][:]),
                                              reads=[cs_stage[l]]))
                    elif tok0 + tw == SEQ:
                        out_toks.append(k.dma("sp", osrc, lambda h: h.dma_start(out=nhp_d[l], in_=hstate[l][:]),
                                              reads=[hstate[l]]))
                        out_toks.append(k.dma("sp", osrc, lambda h: h.dma_start(out=ncp_d[l], in_=hist[l][:]),
                                              reads=[hist[l]]))
                for s, (kind, tok0, tw) in tl:
                    rms_norm(s, tw, vb + V_NMLP, "xn")
                ahead = (l == DEPTH - 1 and nxt is not None)
                if ahead:
                    alt = gv[0:4] + xcb_.items[0:4]
                    nkind, ntok0, ntw = nxt
                    for c_ in range(NCH):
                        k.dma("sp", xsrc[0][c_], lambda h, c_=c_: h.dma_start(out=alt[c_][:, :ntw], in_=xp_d[:, c_, ntok0:ntok0 + ntw]),
                              writes=[alt[c_]])
                    nxt_s = len(groups[gi + 1]) > 1
                    if nxt_s:
                        for c_ in range(NCH):
                            k.dma("sp", xsrc[1][c_], lambda h, c_=c_: h.dma_start(out=x[1][c_][:, :128], in_=xs_d[:, c_, :]),
                                  writes=[x[1][c_]])
                for q in range(4):
                    for half in range(2):
                        u1 = get_unit(("w1", l, 2 * q + half))
                        pre = {}
                        if q == 0 and half == 0:
                            for s, (kind, tok0, tw) in tl:
                                fresh[s] = 0
                                bks = [banks.next() for _ in range(4)]
                                for kk in range(NCH):
                                    for jj in range(4):
                                        k.op("pe", lambda h, kk=kk, jj=jj, s=s, tw=tw, bks=bks: h.matmul(
                                            bks[jj][:, :tw], lhsT=u1[2][:, kk, jj * 128:(jj + 1) * 128], rhs=xn[s][kk][:, :tw],
                                            start=(kk == 0), stop=(kk == NCH - 1)),
                                            reads=[u1[1], xn[s][kk]], writes=[bks[jj]])
                                for jj in range(4):
                                    pre[(jj, s)] = bks[jj]
                        for jj in range(4):
                            j = half * 4 + jj
                            for s, (kind, tok0, tw) in tl:
                                if (jj, s) in pre:
                                    bk = pre[(jj, s)]
                                else:
                                    bk = banks.next()
                                    mm_xn(bk[:, :tw], u1[2], u1[1], jj * 128, s, tw, bk)
                                hj = hid[s][j]
                                k.op("act", lambda h, bk=bk, hj=hj, tw=tw: h.activation(out=hj[:, :tw], in_=bk[:, :tw], func=AF.Relu),
                                     reads=[bk], writes=[hj])
                                k.op("dve", lambda h, hj=hj, tw=tw: h.tensor_tensor(out=hj[:, :tw], in0=hj[:, :tw], in1=hj[:, :tw], op=ALU.mult),
                                     reads=[hj], writes=[hj])
                        release(u1[0])
                    u2a = get_unit(("w2", l, 2 * q))
                    u2b = get_unit(("w2", l, 2 * q + 1))
                    for m in range(NCH):
                        if ahead and q == 3 and m == 4:
                            rms_norm(0, ntw, V_NMIX, "xn", xt=alt)
                            if nxt_s:
                                rms_norm(1, 128, V_NMIX, "xn")
                        for s, (kind, tok0, tw) in tl:
                            bk = banks.next()
                            pairs = []
                            for kk in range(8):
                                uw = u2a if kk < 4 else u2b
                                pairs.append((uw[2][:, kk % 4, m * 128:(m + 1) * 128], hid[s][kk][:, :tw]))
                            if m == 0:
                                mm_group(bk[:, :tw], pairs, None, bk,
                                         split_reads=[[(u2a if kk < 4 else u2b)[1], hid[s][kk]] for kk in range(8)])
                            else:
                                mm_group(bk[:, :tw], pairs, [u2a[1], u2b[1]] + hid[s], bk)
                            k.op("dve", lambda h, bk=bk, s=s, m=m, tw=tw: h.tensor_tensor(
                                out=x[s][m][:, :tw], in0=x[s][m][:, :tw], in1=bk[:, :tw], op=ALU.add),
                                reads=[x[s][m], bk], writes=[x[s][m]])
                    release(u2a[0])
                    release(u2b[0])
            for s, ti in enumerate(g):
                kind, tok0, tw = TILES[ti]
                rms_norm(s, tw, V_FIN, "y", tile_info=TILES[ti])
            if nxt is not None:
                old_x = x[0]
                x[0] = alt
                for i_ in range(4):
                    gv[i_] = old_x[4 + i_]
                xcb_.items = old_x[0:4]
                xcb_.i = 0
                prenormed = True
        k.wait_all("sp", out_toks)
    return nc


def _fm(a):
    a = np.asarray(a, np.float32)
    lead = a.shape[:-1]
    return np.ascontiguousarray(np.moveaxis(a.reshape(lead + (NCH, 128)), -1, 0))


_NC_CACHE = {}


def kernel(x_prompt, x_sample, state_lru_h, state_conv, norm_mix_g, w_in, conv_w, conv_b,
           gate_r_w, gate_r_b, gate_i_w, gate_i_b, lru_lambda, sgu_norm_g, sgu_norm_b, sgu_w, sgu_b,
           w_out, norm_mlp_g, mlp_w1, mlp_w2, final_norm_g):
    f = lambda a: np.ascontiguousarray(np.asarray(a, np.float32))
    x_prompt, x_sample, state_lru_h, state_conv = map(f, (x_prompt, x_sample, state_lru_h, state_conv))
    w_in, w_out, mlp_w1, mlp_w2 = map(f, (w_in, w_out, mlp_w1, mlp_w2))
    sgu_w = f(sgu_w); sgu_b = f(sgu_b)

    vecs = np.zeros((128, NV), np.float32)
    for l in range(DEPTH):
        b = l * LV
        vecs[:, b + V_NMIX:b + V_NMIX + 8] = _fm(norm_mix_g[l])
        vecs[:, b + V_NMLP:b + V_NMLP + 8] = _fm(norm_mlp_g[l])
        for kk in range(4):
            vecs[:, b + V_CW + 8 * kk:b + V_CW + 8 * kk + 8] = _fm(np.asarray(conv_w)[l, kk])
        vecs[:, b + V_CB:b + V_CB + 8] = _fm(conv_b[l])
        vecs[:, b + V_RB:b + V_RB + 8] = _fm(gate_r_b[l])
        vecs[:, b + V_IB:b + V_IB + 8] = _fm(gate_i_b[l])
        vecs[:, b + V_LAM:b + V_LAM + 8] = _fm(lru_lambda[l])
    vecs[:, V_FIN:V_FIN + 8] = _fm(final_norm_g)

    lnp = np.zeros((DEPTH, 2, 128, DSGU), np.float32)
    lnp[:, 0] = np.asarray(sgu_norm_g, np.float32)[:, None, :]
    lnp[:, 1] = np.asarray(sgu_norm_b, np.float32)[:, None, :]

    sw = np.zeros((DEPTH, 2, 128, 8, 128), np.float32)
    sw[:, 0] = np.transpose(sgu_w, (0, 3, 1, 2))
    blk = np.transpose(sgu_w[:, :, :8, :8], (0, 3, 1, 2))
    for b_ in range(16):
        sw[:, 1, 8 * b_:8 * b_ + 8, :, 8 * b_:8 * b_ + 8] = blk
    brow = np.zeros((DEPTH, 2, 40, 128), np.float32)
    for r0 in (0, 32):
        brow[:, 0, r0:r0 + 8] = sgu_b
        brow[:, 1, r0:r0 + 8] = np.tile(sgu_b[:, :, :8], (1, 1, 16))
    sel40 = np.zeros((40, 8, 64), np.float32)
    for h_ in range(8):
        sel40[h_, h_, :] = 1.0
        sel40[32 + h_, h_, :] = 1.0

    gbd = np.zeros((DEPTH, 2, 128, NCH, 128), np.float32)
    for gi, gw in enumerate((np.asarray(gate_r_w, np.float32), np.asarray(gate_i_w, np.float32))):
        for c in range(NCH):
            for bb in range(2):
                gbd[:, gi, 64 * bb:64 * bb + 64, c, 64 * bb:64 * bb + 64] = gw[:, 2 * c + bb]

    if "nc" not in _NC_CACHE:
        _NC_CACHE["nc"] = build_program()
    nc = _NC_CACHE["nc"]

    in_maps = []
    for core in range(NCORES):
        xp = np.ascontiguousarray(x_prompt[core].T.reshape(NCH, 128, SEQ).transpose(1, 0, 2))
        xs = x_sample[16 * core:16 * core + 16].reshape(128, D)
        xs = np.ascontiguousarray(xs.T.reshape(NCH, 128, 128).transpose(1, 0, 2))
        h0 = state_lru_h[:, 16 * core:16 * core + 16]
        h0s = np.ascontiguousarray(h0.reshape(DEPTH, 16, NCH, 128).transpose(0, 3, 2, 1))
        cv = state_conv[:, 16 * core:16 * core + 16]
        cv0s = np.ascontiguousarray(cv.reshape(DEPTH, 16, 3, NCH, 128).transpose(0, 4, 3, 1, 2))
        in_maps.append({
            "xp": xp, "xs": xs, "h0s": h0s, "cv0s": cv0s, "vecs": vecs, "lnp": lnp, "brow": brow, "sel40": sel40,
            "sw": sw, "gbd": gbd, "w_in": w_in, "w_out": w_out, "w1": mlp_w1, "w2": mlp_w2,
        })
    res = run_bass_kernel_spmd(nc, in_maps, core_ids=list(range(NCORES)))
    R = res.results

    y_prompt = np.stack([R[c]["yp"].transpose(1, 0, 2).reshape(D, SEQ).T for c in range(NCORES)])
    y_sample = np.concatenate([R[c]["ys"].transpose(1, 0, 2).reshape(D, 128).T.reshape(16, 8, D) for c in range(NCORES)])
    nhp = np.stack([R[c]["nhp"].transpose(0, 2, 1).reshape(DEPTH, D) for c in range(NCORES)], axis=1)
    ncp = np.stack([R[c]["ncp"].transpose(0, 3, 2, 1).reshape(DEPTH, 3, D) for c in range(NCORES)], axis=1)
    nhs = np.concatenate([R[c]["nhs"].transpose(0, 3, 2, 1).reshape(DEPTH, 16, D) for c in range(NCORES)], axis=1)
    ncs = np.concatenate([R[c]["ncs"].transpose(0, 3, 4, 2, 1).reshape(DEPTH, 16, 3, D) for c in range(NCORES)], axis=1)
    nvs = np.concatenate([R[c]["nvs"].reshape(DEPTH, 16, 8, DSGU) for c in range(NCORES)], axis=1)
    out = (y_prompt, y_sample, nhp, ncp, nhs, ncs, nvs)
    return tuple(np.ascontiguousarray(o, dtype=np.float32) for o in out)
```

```python
import contextlib

import numpy as np
import concourse.bass as bass
import concourse.mybir as mybir
from concourse.bass_utils import run_bass_kernel_spmd

F32 = mybir.dt.float32
BF16 = mybir.dt.bfloat16
AF = mybir.ActivationFunctionType
ALU = mybir.AluOpType

NCORES = 8
D = 1024
NCH = 8
DSGU = 512
DFF = 4096
DEPTH = 2
SEQ = 2048
EPS = 1e-6
TILES = [("p", 0, 512), ("p", 512, 512), ("p", 1024, 512), ("p", 1536, 512), ("s", 0, 128)]
GROUPS = [[0], [1], [2], [3, 4]]
NSLOT = 6
UNIT = 4096
N_WARM = 10

LV = 80
V_NMIX, V_NMLP, V_CW, V_CB, V_RB, V_IB, V_LAM = 0, 8, 16, 48, 56, 64, 72
V_FIN = 160
NV = 168


class Src:
    def __init__(self, name, sem):
        self.name = name
        self.sem = sem
        self.count = 0


class T:
    def __init__(self, name, handle, excl=False):
        self.name = name
        self.h = handle
        self.excl = excl
        self.w = None
        self.r = []

    def __getitem__(self, idx):
        return self.h[idx]


class Eng:
    def __init__(self, name, handle, src):
        self.name = name
        self.h = handle
        self.src = src
        self.known = {}


class K:
    def __init__(self, nc, stack):
        self.nc = nc
        self.stack = stack
        self.engs = {}
        for name, h in (("pe", nc.tensor), ("act", nc.scalar), ("dve", nc.vector),
                        ("pool", nc.gpsimd), ("sp", nc.sync)):
            self.engs[name] = Eng(name, h, Src(name, self._sem("e_" + name)))

    def _sem(self, name):
        return self.stack.enter_context(self.nc.semaphore(name))

    def dma_src(self, name):
        return Src(name, self._sem("d_" + name))

    def sb(self, name, shape, dtype):
        return T(name, self.stack.enter_context(self.nc.sbuf_tensor("s_" + name, list(shape), dtype)))

    def ps(self, name, shape, dtype):
        return T(name, self.stack.enter_context(self.nc.psum_tensor("p_" + name, list(shape), dtype)), excl=True)

    def _sync(self, e, reads, writes):
        deps = {}

        def add(sv):
            if sv is None:
                return
            s, v = sv
            if e.name == "pe" and s is e.src:
                return
            if deps.get(s, 0) < v:
                deps[s] = v

        for t in reads:
            add(t.w)
            if t.excl:
                for sv in t.r:
                    add(sv)
        for t in writes:
            add(t.w)
            for sv in t.r:
                add(sv)
        for s, v in deps.items():
            if e.known.get(s, 0) < v:
                e.h.wait_ge(s.sem, v)
                e.known[s] = v

    @staticmethod
    def _mark(tok, reads, writes):
        for t in reads:
            t.r.append(tok)
            if len(t.r) > 12:
                best = {}
                for s, v in t.r:
                    if best.get(s, 0) < v:
                        best[s] = v
                t.r = list(best.items())
        for t in writes:
            t.w = tok
            t.r = []

    def op(self, eng, fn, reads=(), writes=()):
        e = self.engs[eng]
        self._sync(e, reads, writes)
        ins = fn(e.h)
        e.src.count += 1
        ins.then_inc(e.src.sem, 1)
        tok = (e.src, e.src.count)
        self._mark(tok, reads, writes)
        return tok

    def dma(self, q, src, fn, reads=(), writes=()):
        e = self.engs[q]
        self._sync(e, reads, writes)
        inss = fn(e.h)
        if not isinstance(inss, (list, tuple)):
            inss = [inss]
        for ins in inss:
            src.count += 16
            ins.then_inc(src.sem, 16)
        tok = (src, src.count)
        self._mark(tok, reads, writes)
        return tok

    def wait_all(self, eng, toks):
        e = self.engs[eng]
        best = {}
        for s, v in toks:
            if best.get(s, 0) < v:
                best[s] = v
        for s, v in best.items():
            if e.known.get(s, 0) < v:
                e.h.wait_ge(s.sem, v)
                e.known[s] = v


class Rot:
    def __init__(self, items):
        self.items = items
        self.i = 0

    def next(self):
        t = self.items[self.i % len(self.items)]
        self.i += 1
        return t


def build_program(groups=GROUPS):
    nc = bass.Bass("TRN2", target_bir_lowering=False)

    def din(name, shape):
        return nc.dram_tensor(name, list(shape), F32, kind="ExternalInput").ap()

    def dout(name, shape):
        return nc.dram_tensor(name, list(shape), F32, kind="ExternalOutput").ap()

    xp_d = din("xp", [128, NCH, SEQ])
    xs_d = din("xs", [128, NCH, 128])
    h0s_d = din("h0s", [DEPTH, 128, NCH, 16])
    cv0s_d = din("cv0s", [DEPTH, 128, NCH, 16, 3])
    vecs_d = din("vecs", [128, NV])
    lnp_d = din("lnp", [DEPTH, 2, 128, DSGU])
    brow_d = din("brow", [DEPTH, 2, 40, 128])
    sel_d = din("sel40", [40, 8, 64])
    sw_d = din("sw", [DEPTH, 2, 128, 8, 128])
    gbd_d = din("gbd", [DEPTH, 2, 128, NCH, 128])
    w_in_d = din("w_in", [DEPTH, D, 3 * D])
    w_out_d = din("w_out", [DEPTH, 1536, D])
    w1_d = din("w1", [DEPTH, D, DFF])
    w2_d = din("w2", [DEPTH, DFF, D])

    yp_d = dout("yp", [128, NCH, SEQ])
    ys_d = dout("ys", [128, NCH, 128])
    nhp_d = dout("nhp", [DEPTH, 128, NCH])
    ncp_d = dout("ncp", [DEPTH, 128, NCH, 3])
    nhs_d = dout("nhs", [DEPTH, 128, NCH, 16])
    ncs_d = dout("ncs", [DEPTH, 128, NCH, 16, 3])
    nvs_d = dout("nvs", [DEPTH, 128, DSGU])

    with contextlib.ExitStack() as st:
        k = K(nc, st)
        out_toks = []

        WS = [512, 128]
        x = [[k.sb(f"x{s}_{c}", [128, WS[s]], F32) for c in range(NCH)] for s in range(2)]
        xn = [[k.sb(f"xn{s}_{c}", [128, WS[s]], BF16) for c in range(NCH)] for s in range(2)]
        ub = [[k.sb(f"u{s}_{c}", [128, WS[s]], BF16) for c in range(4)] for s in range(2)]
        mix = [[k.sb(f"mix{s}_{c}", [128, WS[s]], BF16) for c in range(12)] for s in range(2)]
        hid = [[k.sb(f"hid{s}_{j}", [128, WS[s]], BF16) for j in range(8)] for s in range(2)]
        xbuf = Rot([k.sb(f"xbuf{i}", [128, 516], F32) for i in range(2)])
        xcb_ = Rot([k.sb(f"xc{i}", [128, 512], F32) for i in range(4)])
        xcbf = Rot([k.sb(f"xcb{i}", [128, 512], BF16) for i in range(4)])
        Rb = Rot([k.sb(f"R{i}", [128, 512], F32) for i in range(2)])
        Ab = Rot([k.sb(f"A{i}", [128, 512], F32) for i in range(3)])
        Ib = Rot([k.sb(f"I{i}", [128, 512], F32) for i in range(3)])
        Mb = Rot([k.sb(f"M{i}", [128, 512], F32) for i in range(2)])
        hb = Rot([k.sb(f"h{i}", [128, 512], F32) for i in range(2)])
        gv = [k.sb(f"gv{i}", [128, DSGU], F32) for i in range(5)]
        vnb = [k.sb(f"vnb{i}", [128, DSGU], BF16) for i in range(5)]
        vo = k.sb("vo", [128, DSGU], F32)
        stats = k.sb("stats", [128, 5, 6], F32)
        mv = k.sb("mv", [128, 5, 2], F32)
        lnr = k.sb("lnr", [128, 5], F32)
        sqb = Rot([k.sb(f"sqb{i}", [128, 512], BF16) for i in range(3)])
        rstd = Rot([k.sb(f"rstd{i}", [128, 512], F32) for i in range(2)])
        banks = Rot([k.ps(f"bank{i}", [128, 512], F32) for i in range(8)])
        ring = [k.sb(f"ring{i}", [128, UNIT], BF16) for i in range(NSLOT)]
        ring_src = [k.dma_src(f"ring{i}") for i in range(NSLOT)]

        vecs = k.sb("vecs", [128, NV], F32)
        c8 = k.sb("c8", [128, 16], F32)
        hc8 = k.sb("hc8", [128, 16], F32)
        hrb = k.sb("hrb", [128, 16], F32)
        hib = k.sb("hib", [128, 16], F32)
        lnp = [k.sb(f"lnp{i}", [128, DSGU], F32) for i in range(2)]
        lnp_src = k.dma_src("lnp")
        swT = [[k.sb(f"swT{l}{i}", [128, 8, 128], BF16) for i in range(2)] for l in range(DEPTH)]
        gbd = [[k.sb(f"gbd{l}{i}", [128, NCH, 128], BF16) for i in range(2)] for l in range(DEPTH)]
        brow = [[k.sb(f"brow{l}{i}", [40, 128], F32) for i in range(2)] for l in range(DEPTH)]
        bhl = [[k.sb(f"bhl{l}{i}", [40, 128], BF16) for i in range(2)] for l in range(DEPTH)]
        bhi_t = k.sb("bhi_t", [40, 128], BF16)
        btmp = k.sb("btmp", [40, 128], F32)
        sel = k.sb("sel", [40, 8, 64], BF16)
        onesb = k.sb("onesb", [128, 128], BF16)
        onesb_w = k.sb("onesb_w", [128, 512], BF16)
        hstate = [k.sb(f"hstate{l}", [128, NCH], F32) for l in range(DEPTH)]
        hist = [k.sb(f"hist{l}", [128, NCH, 3], F32) for l in range(DEPTH)]
        h0s = [k.sb(f"h0s{l}", [128, NCH, 16], F32) for l in range(DEPTH)]
        cv0s = [k.sb(f"cv0s{l}", [128, NCH, 16, 3], F32) for l in range(DEPTH)]
        hs_stage = [k.sb(f"hsst{l}", [128, NCH, 16], F32) for l in range(DEPTH)]
        cs_stage = [k.sb(f"csst{l}", [128, NCH, 16, 3], F32) for l in range(DEPTH)]
        tmp16 = k.sb("tmp16", [128, 16], F32)

        csrc = k.dma_src("const")
        csrc_sw = k.dma_src("const_sw")
        xsrc = [[k.dma_src(f"xin{s}_{c}") for c in range(NCH)] for s in range(2)]
        osrc = k.dma_src("out")
        ysrc = [k.dma_src(f"yout{c}") for c in range(NCH)]
        vo.dsrc = k.dma_src("vo")

        def unit_list(l):
            us = []
            for cb in (2, 4, 1, 0, 3, 5):
                us.append(("in", l, cb))
            for ob in range(4):
                us.append(("out", l, ob))
            for q in range(4):
                us += [("w1", l, 2 * q), ("w1", l, 2 * q + 1), ("w2", l, 2 * q), ("w2", l, 2 * q + 1)]
            return us

        seq = []
        for _g in groups:
            for l in range(DEPTH):
                seq += unit_list(l)
        st_ = {"next": 0, "free": list(range(NSLOT)), "live": {}}

        def unit_view(key, slot):
            t = ring[slot]
            kind = key[0]
            if kind in ("in", "w1"):
                return t[:, 0:8 * 512].rearrange("p (k n) -> p k n", n=512)
            if kind == "out":
                return t[:, 0:12 * 256].rearrange("p (k n) -> p k n", n=256)
            return t[:, 0:4 * 1024].rearrange("p (k n) -> p k n", n=1024)

        def unit_src(key):
            kind, l, i = key
            if kind == "in":
                return w_in_d[l].rearrange("(k p) n -> p k n", p=128)[:, :, i * 512:(i + 1) * 512]
            if kind == "out":
                return w_out_d[l].rearrange("(k p) n -> p k n", p=128)[:, :, i * 256:(i + 1) * 256]
            if kind == "w1":
                return w1_d[l].rearrange("(k p) n -> p k n", p=128)[:, :, i * 512:(i + 1) * 512]
            return w2_d[l, i * 512:(i + 1) * 512, :].rearrange("(k p) n -> p k n", p=128)

        def prefetch():
            while st_["next"] < len(seq) and st_["free"]:
                idx = st_["next"]
                key = seq[idx]
                slot = st_["free"].pop(0)
                st_["next"] += 1
                view = unit_view(key, slot)
                k.dma("pool", ring_src[slot], lambda h, v=view, s=unit_src(key): h.dma_start(out=v, in_=s),
                      writes=[ring[slot]])
                st_["live"][idx] = slot

        pos = {"i": 0}

        def get_unit(key):
            idx = pos["i"]
            assert seq[idx] == key, (seq[idx], key)
            pos["i"] += 1
            prefetch()
            assert idx in st_["live"], "ring too small for access pattern"
            slot = st_["live"][idx]
            return idx, ring[slot], unit_view(key, slot)

        def release(idx):
            slot = st_["live"].pop(idx)
            st_["free"].append(slot)
            prefetch()

        k.op("pool", lambda h: h.memset(onesb[:], 1.0 / D), writes=[onesb])
        k.op("pool", lambda h: h.memset(onesb_w[:], 0.5), writes=[onesb_w])
        kind0, tok00, tw0 = TILES[groups[0][0]]
        for c_ in range(NCH):
            k.dma("sp", xsrc[0][c_], lambda h, c_=c_: h.dma_start(out=x[0][c_][:, :tw0], in_=xp_d[:, c_, tok00:tok00 + tw0]),
                  writes=[x[0][c_]])
        const_tiles = []

        def ld(q, src, dst_t, dst_ap, src_ap):
            const_tiles.append((dst_t, src))
            return k.dma(q, src, lambda h: h.dma_start(out=dst_ap, in_=src_ap), writes=[dst_t])

        vsrc = k.dma_src("vecs")
        k.dma("sp", vsrc, lambda h: h.dma_start(out=vecs[:], in_=vecs_d), writes=[vecs])
        ld("pool", csrc_sw, sel, sel[:], sel_d)
        for l in range(DEPTH):
            for i in range(2):
                ld("sp", csrc, brow[l][i], brow[l][i][:], brow_d[l, i])
                ld("pool", csrc_sw, swT[l][i], swT[l][i][:], sw_d[l, i])
                ld("pool", csrc_sw, gbd[l][i], gbd[l][i][:], gbd_d[l, i])
            ld("sp", csrc, h0s[l], h0s[l][:], h0s_d[l])
            ld("sp", csrc, cv0s[l], cv0s[l][:], cv0s_d[l])
        for t_, src_ in const_tiles:
            t_.w = (src_, src_.count)
        def mask_prep():
            for l in range(DEPTH):
                for i in range(2):
                    k.op("pool", lambda h, t=swT[l][i]: h.affine_select(
                        out=t[:], in_=t[:], pattern=[[0, 8], [1, 128]], compare_op=ALU.is_ge, fill=0.0,
                        base=0, channel_multiplier=-1), reads=[swT[l][i]], writes=[swT[l][i]])

        def const_prep():
            for l in range(DEPTH):
                k.op("dve", lambda h, l=l: h.memset(hstate[l][:], 0.0), writes=[hstate[l]])
                k.op("dve", lambda h, l=l: h.memset(hist[l][:], 0.0), writes=[hist[l]])
                for i in range(2):
                    k.op("dve", lambda h, l=l, i=i: h.tensor_copy(out=bhi_t[:], in_=brow[l][i][:]),
                         reads=[brow[l][i]], writes=[bhi_t])
                    k.op("dve", lambda h: h.tensor_copy(out=btmp[:], in_=bhi_t[:]), reads=[bhi_t], writes=[btmp])
                    k.op("dve", lambda h, l=l, i=i: h.tensor_tensor(out=btmp[:], in0=brow[l][i][:], in1=btmp[:], op=ALU.subtract),
                         reads=[brow[l][i], btmp], writes=[btmp])
                    k.op("dve", lambda h, l=l, i=i: h.memset(bhl[l][i][:], 0.0), writes=[bhl[l][i]])
                    k.op("dve", lambda h, l=l, i=i: h.tensor_copy(out=bhl[l][i][0:8, :], in_=bhi_t[0:8, :]),
                         reads=[bhi_t], writes=[bhl[l][i]])
                    k.op("dve", lambda h, l=l, i=i: h.tensor_copy(out=bhl[l][i][32:40, :], in_=btmp[32:40, :]),
                         reads=[btmp], writes=[bhl[l][i]])
            for l in range(DEPTH):
                lam = vecs[:, l * LV + V_LAM:l * LV + V_LAM + 8]
                o = c8[:, l * 8:(l + 1) * 8]
                k.op("act", lambda h, lam=lam, o=o: h.activation(out=o, in_=lam, func=AF.Exp, scale=-1.0),
                     reads=[vecs], writes=[c8])
                k.op("act", lambda h, o=o: h.activation(out=o, in_=o, func=AF.Ln, bias=1.0), reads=[c8], writes=[c8])
                k.op("dve", lambda h, l=l, o=o: h.tensor_scalar(out=hc8[:, l * 8:(l + 1) * 8], in0=o, scalar1=-4.0,
                                                               scalar2=None, op0=ALU.mult), reads=[c8], writes=[hc8])
                k.op("dve", lambda h, o=o: h.tensor_scalar(out=o, in0=o, scalar1=-8.0, scalar2=None, op0=ALU.mult),
                     reads=[c8], writes=[c8])
                k.op("dve", lambda h, l=l: h.tensor_scalar(out=hrb[:, l * 8:(l + 1) * 8],
                                                          in0=vecs[:, l * LV + V_RB:l * LV + V_RB + 8],
                                                          scalar1=0.5, scalar2=None, op0=ALU.mult),
                     reads=[vecs], writes=[hrb])
                k.op("dve", lambda h, l=l: h.tensor_scalar(out=hib[:, l * 8:(l + 1) * 8],
                                                          in0=vecs[:, l * LV + V_IB:l * LV + V_IB + 8],
                                                          scalar1=0.5, scalar2=None, op0=ALU.mult),
                     reads=[vecs], writes=[hib])

        def mm_group(out_ap, pairs, reads, bank, split_reads=None):
            if split_reads is not None:
                n = len(pairs)
                tok = None
                for i, (lt, rh) in enumerate(pairs):
                    tok = k.op("pe", lambda h, lt=lt, rh=rh, i=i: h.matmul(out_ap, lhsT=lt, rhs=rh, start=(i == 0),
                                                                          stop=(i == n - 1)),
                               reads=split_reads[i], writes=[bank])
                return tok

            def fn(h):
                ins = None
                n = len(pairs)
                for i, (lt, rh) in enumerate(pairs):
                    ins = h.matmul(out_ap, lhsT=lt, rhs=rh, start=(i == 0), stop=(i == n - 1))
                return ins
            return k.op("pe", fn, reads=reads, writes=[bank])

        fresh = {0: 0, 1: 0}

        def mm_xn(out_ap, wview, wtile, col0, s, tw, bank):
            pairs = [(wview[:, kk, col0:col0 + 128], xn[s][kk][:, :tw]) for kk in range(NCH)]
            if fresh[s] > 0:
                fresh[s] -= 1
                return mm_group(out_ap, pairs, None, bank, split_reads=[[wtile, xn[s][kk]] for kk in range(NCH)])
            return mm_group(out_ap, pairs, [wtile] + xn[s], bank)

        def rms_norm(s, tw, gcol, dst_kind, tile_info=None, xt=None):
            xt = x[s] if xt is None else xt
            if dst_kind == "xn":
                fresh[s] = 2
            bank = banks.next()
            for c in range(NCH):
                sq = sqb.next()
                k.op("act", lambda h, c=c, sq=sq: h.activation(out=sq[:, :tw], in_=xt[c][:, :tw], func=AF.Square),
                     reads=[xt[c]], writes=[sq])
                k.op("pe", lambda h, c=c, sq=sq: h.matmul(bank[:, :tw], lhsT=onesb[:], rhs=sq[:, :tw],
                                                          start=(c == 0), stop=(c == NCH - 1)),
                     reads=[onesb, sq], writes=[bank])
            if dst_kind == "xn" and tw == 512 and N_WARM > 0:
                wb = banks.next()
                k.op("pe", lambda h: [h.matmul(wb[:, :tw], lhsT=onesb[:], rhs=onesb_w[:, :tw], start=True, stop=True)
                                      for _ in range(N_WARM)][-1], reads=[onesb, onesb_w], writes=[wb])
            rs = rstd.next()
            k.op("act", lambda h: h.activation(out=rs[:, :tw], in_=bank[:, :tw], func=AF.Ln, bias=EPS),
                 reads=[bank], writes=[rs])
            k.op("act", lambda h: h.activation(out=rs[:, :tw], in_=rs[:, :tw], func=AF.Exp, scale=-0.5),
                 reads=[rs], writes=[rs])
            for c in range(NCH):
                g = vecs[:, gcol + c:gcol + c + 1]
                if dst_kind == "xn":
                    k.op("dve", lambda h, c=c, g=g: h.scalar_tensor_tensor(
                        out=xn[s][c][:, :tw], in0=xt[c][:, :tw], scalar=g, in1=rs[:, :tw],
                        op0=ALU.mult, op1=ALU.mult), reads=[xt[c], vecs, rs], writes=[xn[s][c]])
                else:
                    k.op("dve", lambda h, c=c, g=g: h.scalar_tensor_tensor(
                        out=xt[c][:, :tw], in0=xt[c][:, :tw], scalar=g, in1=rs[:, :tw],
                        op0=ALU.mult, op1=ALU.mult), reads=[xt[c], vecs, rs], writes=[xt[c]])
                    kind, tok0, _ = tile_info
                    dst = yp_d[:, c, tok0:tok0 + tw] if kind == "p" else ys_d[:, c, :]
                    ysrc_c = ysrc[c] if kind == "p" else osrc
                    out_toks.append(k.dma("sp", ysrc_c, lambda h, dst=dst, c=c: h.dma_start(out=dst, in_=xt[c][:, :tw]),
                                          reads=[xt[c]]))

        def v3(ap, tw):
            return ap.rearrange("p (b t) -> p b t", t=8)

        prenormed = False
        for gi, g in enumerate(groups):
            tl = [(s, TILES[ti]) for s, ti in enumerate(g)]
            nxt = TILES[groups[gi + 1][0]] if gi + 1 < len(groups) else None
            for s, (kind, tok0, tw) in tl:
                if (s == 0 and (prenormed or gi == 0)) or (s == 1 and prenormed):
                    continue
                src_ap = xp_d[:, :, tok0:tok0 + tw] if kind == "p" else xs_d
                for c_ in range(NCH):
                    k.dma("sp", xsrc[s][c_], lambda h, s=s, tw=tw, a=src_ap, c_=c_: h.dma_start(out=x[s][c_][:, :tw], in_=a[:, c_, :]),
                          writes=[x[s][c_]])
            for l in range(DEPTH):
                vb = l * LV
                k.dma("sp", lnp_src, lambda h: [h.dma_start(out=lnp[i_][:], in_=lnp_d[l, i_]) for i_ in range(2)],
                      writes=[lnp[0], lnp[1]])
                for s, (kind, tok0, tw) in tl:
                    if l == 0 and prenormed:
                        continue
                    rms_norm(s, tw, vb + V_NMIX, "xn")
                if gi == 0 and l == 0:
                    const_prep()
                lst = {}
                uxs = {}
                uys = {}
                pre_xb = {}
                pre_yb = {}
                pending_casts = []

                def flush_casts():
                    while pending_casts:
                        xcf, xc, tw = pending_casts.pop(0)
                        k.op("act", lambda h, xcf=xcf, xc=xc, tw=tw: h.copy(out=xcf[:, :tw], in_=xc[:, :tw]),
                             reads=[xc], writes=[xcf])

                def stage_a(items):
                    st_a = []
                    for (c, s, kind, tok0, tw) in items:
                        half, cc = divmod(c, 4)
                        if half not in uxs:
                            uxs[half] = get_unit(("in", l, 2 + half))
                        ux = uxs[half]
                        if (c, s) in pre_xb:
                            bk = pre_xb.pop((c, s))
                        else:
                            bk = banks.next()
                            mm_xn(bk[:, :tw], ux[2], ux[1], cc * 128, s, tw, bk)
                        xb = xbuf.next()
                        xc = xcb_.next()
                        if kind == "p":
                            k.op("act", lambda h, xb=xb, bk=bk, tw=tw: h.copy(out=xb[:, 3:3 + tw], in_=bk[:, :tw]),
                                 reads=[bk], writes=[xb])
                            xin = [xb[:, kk:kk + tw] for kk in range(4)]
                            xco = xc[:, :tw]
                            xbv = None
                        else:
                            xbv = xb[:, 0:176].rearrange("p (b q) -> p b q", q=11)
                            k.op("act", lambda h, xbv=xbv, bk=bk, tw=tw: h.copy(out=xbv[:, :, 3:11], in_=v3(bk[:, :tw], tw)),
                                 reads=[bk], writes=[xb])
                            xin = [xbv[:, :, kk:kk + 8] for kk in range(4)]
                            xco = v3(xc[:, :tw], tw)
                        st_a.append((c, s, kind, tw, xb, xc, xbv, xin, xco))
                    for (c, s, kind, tw, xb, xc, xbv, xin, xco) in st_a:
                        if kind == "p":
                            k.op("act", lambda h, xb=xb, c=c: h.copy(out=xb[:, 0:3], in_=hist[l][:, c, :]),
                                 reads=[hist[l]], writes=[xb])
                        else:
                            k.op("act", lambda h, xbv=xbv, c=c: h.copy(out=xbv[:, :, 0:3], in_=cv0s[l][:, c, :, :]),
                                 reads=[cv0s[l]], writes=[xb])
                    for kk in (3, 0, 1, 2):
                        for (c, s, kind, tw, xb, xc, xbv, xin, xco) in st_a:
                            cwk = vecs[:, vb + V_CW + 8 * kk + c:vb + V_CW + 8 * kk + c + 1]
                            if kk == 3:
                                cbias = vecs[:, vb + V_CB + c:vb + V_CB + c + 1]
                                k.op("dve", lambda h, xin=xin, xco=xco, cwk=cwk, cbias=cbias: h.tensor_scalar(
                                    out=xco, in0=xin[3], scalar1=cwk, scalar2=cbias, op0=ALU.mult, op1=ALU.add),
                                    reads=[xb, vecs], writes=[xc])
                            else:
                                k.op("dve", lambda h, xin=xin, xco=xco, cwk=cwk, kk=kk: h.scalar_tensor_tensor(
                                    out=xco, in0=xin[kk], scalar=cwk, in1=xco, op0=ALU.mult, op1=ALU.add),
                                    reads=[xb, vecs, xc], writes=[xc])
                    for (c, s, kind, tw, xb, xc, xbv, xin, xco) in st_a:
                        if kind == "p":
                            k.op("act", lambda h, xb=xb, c=c, tw=tw: h.copy(out=hist[l][:, c, :], in_=xb[:, tw:tw + 3]),
                                 reads=[xb], writes=[hist[l]])
                        else:
                            k.op("act", lambda h, xbv=xbv, c=c: h.copy(out=cs_stage[l][:, c, :, :], in_=xbv[:, :, 8:11]),
                                 reads=[xb], writes=[cs_stage[l]])
                        xcf = xcbf.next()
                        pending_casts.append((xcf, xc, tw))
                        lst[(c, s)] = {"xc": xc, "xcf": xcf}
                        if c % 4 == 3 and s == len(tl) - 1:
                            release(uxs[c // 4][0])

                def b_pe(items):
                    sb_ = []
                    for (c, s, kind, tok0, tw) in items:
                        d = lst[(c, s)]
                        xc, xcf = d["xc"], d["xcf"]
                        bR = banks.next()
                        mm_group(bR[:, :tw], [(gbd[l][0][:, c, :], xcf[:, :tw])], [gbd[l][0], xcf], bR)
                        bI = banks.next()
                        mm_group(bI[:, :tw], [(gbd[l][1][:, c, :], xcf[:, :tw])], [gbd[l][1], xcf], bI)
                        sb_.append(dict(c=c, s=s, kind=kind, tw=tw, xc=xc, bR=bR, bI=bI, R=Rb.next(), A=Ab.next(),
                                        I=Ib.next(), M=Mb.next(), hh=hb.next(), col=l * 8 + c))
                    return sb_

                def b_rest(sb_):
                    for e in sb_:
                        k.op("act", lambda h, e=e: h.activation(
                            out=e["R"][:, :e["tw"]], in_=e["bR"][:, :e["tw"]], func=AF.Tanh,
                            bias=hrb[:, e["col"]:e["col"] + 1], scale=0.5), reads=[e["bR"], hrb], writes=[e["R"]])
                    for e in sb_:
                        k.op("act", lambda h, e=e: h.activation(
                            out=e["I"][:, :e["tw"]], in_=e["bI"][:, :e["tw"]], func=AF.Tanh,
                            bias=hib[:, e["col"]:e["col"] + 1], scale=0.5), reads=[e["bI"], hib], writes=[e["I"]])
                    for e in sb_:
                        k.op("act", lambda h, e=e: h.activation(
                            out=e["M"][:, :e["tw"]], in_=e["R"][:, :e["tw"]], func=AF.Exp,
                            bias=c8[:, e["col"]:e["col"] + 1], scale=c8[:, e["col"]:e["col"] + 1]),
                            reads=[e["R"], c8], writes=[e["M"]])
                    for e in sb_:
                        k.op("act", lambda h, e=e: h.activation(out=e["M"][:, :e["tw"]], in_=e["M"][:, :e["tw"]],
                                                               func=AF.Ln, bias=1.0, scale=-1.0),
                             reads=[e["M"]], writes=[e["M"]])
                    for e in sb_:
                        k.op("dve", lambda h, e=e: h.scalar_tensor_tensor(
                            out=e["I"][:, :e["tw"]], in0=e["I"][:, :e["tw"]], scalar=1.0, in1=e["xc"][:, :e["tw"]],
                            op0=ALU.add, op1=ALU.mult), reads=[e["I"], e["xc"]], writes=[e["I"]])
                    for e in sb_:
                        k.op("act", lambda h, e=e: h.activation(out=e["M"][:, :e["tw"]], in_=e["M"][:, :e["tw"]],
                                                               func=AF.Exp, scale=0.5),
                             reads=[e["M"]], writes=[e["M"]])
                    for e in sb_:
                        k.op("act", lambda h, e=e: h.activation(
                            out=e["A"][:, :e["tw"]], in_=e["R"][:, :e["tw"]], func=AF.Exp,
                            bias=hc8[:, e["col"]:e["col"] + 1], scale=hc8[:, e["col"]:e["col"] + 1]),
                            reads=[e["R"], hc8], writes=[e["A"]])
                    for e in sb_:
                        k.op("dve", lambda h, e=e: h.scalar_tensor_tensor(
                            out=e["M"][:, :e["tw"]], in0=e["I"][:, :e["tw"]], scalar=0.5, in1=e["M"][:, :e["tw"]],
                            op0=ALU.mult, op1=ALU.mult), reads=[e["I"], e["M"]], writes=[e["M"]])
                    for e in sb_:
                        tw, A, M, hh, c = e["tw"], e["A"], e["M"], e["hh"], e["c"]
                        if e["kind"] == "s":
                            Av = v3(A[:, :tw], tw); Mv = v3(M[:, :tw], tw)
                            k.op("dve", lambda h, Av=Av, c=c: h.tensor_tensor(
                                out=tmp16[:], in0=Av[:, :, 0], in1=h0s[l][:, c, :], op=ALU.mult),
                                reads=[A, h0s[l]], writes=[tmp16])
                            k.op("dve", lambda h, Mv=Mv: h.tensor_tensor(
                                out=Mv[:, :, 0], in0=Mv[:, :, 0], in1=tmp16[:], op=ALU.add),
                                reads=[M, tmp16], writes=[M])
                            k.op("dve", lambda h, Av=Av: h.memset(Av[:, :, 0], 0.0), reads=[A], writes=[A])
                    for e in sb_:
                        tw, A, M, hh, c = e["tw"], e["A"], e["M"], e["hh"], e["c"]
                        if e["kind"] == "p":
                            k.op("dve", lambda h, hh=hh, A=A, M=M, c=c, tw=tw: h.tensor_tensor_scan(
                                out=hh[:, :tw], data0=A[:, :tw], data1=M[:, :tw], initial=hstate[l][:, c:c + 1],
                                op0=ALU.mult, op1=ALU.add), reads=[A, M, hstate[l]], writes=[hh])
                        else:
                            k.op("dve", lambda h, hh=hh, A=A, M=M, tw=tw: h.tensor_tensor_scan(
                                out=hh[:, :tw], data0=A[:, :tw], data1=M[:, :tw], initial=0.0,
                                op0=ALU.mult, op1=ALU.add), reads=[A, M], writes=[hh])
                    for e in sb_:
                        tw, hh, c, s = e["tw"], e["hh"], e["c"], e["s"]
                        if e["kind"] == "p":
                            k.op("dve", lambda h, hh=hh, c=c, tw=tw: h.tensor_copy(out=hstate[l][:, c:c + 1], in_=hh[:, tw - 1:tw]),
                                 reads=[hh], writes=[hstate[l]])
                        else:
                            k.op("dve", lambda h, hh=hh, c=c, tw=tw: h.tensor_copy(
                                out=hs_stage[l][:, c, :], in_=v3(hh[:, :tw], tw)[:, :, 7]),
                                reads=[hh], writes=[hs_stage[l]])
                    for e in sb_:
                        tw, hh, c, s = e["tw"], e["hh"], e["c"], e["s"]
                        k.op("dve", lambda h, hh=hh, c=c, s=s, tw=tw: h.tensor_tensor(
                            out=mix[s][4 + c][:, :tw], in0=hh[:, :tw], in1=mix[s][4 + c][:, :tw], op=ALU.mult),
                            reads=[hh, mix[s][4 + c]], writes=[mix[s][4 + c]])

                def yb_task(c):
                    half, cc = divmod(c, 4)
                    if half not in uys:
                        uys[half] = get_unit(("in", l, 4 + half))
                    uy = uys[half]
                    for s, (kind, tok0, tw) in tl:
                        if (c, s) in pre_yb:
                            bY = pre_yb.pop((c, s))
                        else:
                            bY = banks.next()
                            mm_group(bY[:, :tw], [(uy[2][:, kk, cc * 128:(cc + 1) * 128], xn[s][kk][:, :tw]) for kk in range(NCH)],
                                     [uy[1]] + xn[s], bY)
                        k.op("act", lambda h, bY=bY, s=s, c=c, tw=tw: h.activation(
                            out=mix[s][4 + c][:, :tw], in_=bY[:, :tw], func=AF.Gelu_apprx_tanh),
                            reads=[bY], writes=[mix[s][4 + c]])
                    if cc == 3:
                        release(uy[0])

                subs = []
                for s, (kind, tok0, tw) in tl:
                    for j in range(tw // 128):
                        subs.append((s, kind, j))
                ns = len(subs)
                uvu = {}

                def v_task(n):
                    s, kind, j = subs[n]
                    uv = uvu["v"]
                    bk = banks.next()
                    mm_group(bk[:, :], [(xn[s][kk][:, j * 128:(j + 1) * 128], uv[2][:, kk, :]) for kk in range(NCH)],
                             [uv[1]] + xn[s], bk)
                    k.op("act", lambda h, n=n, bk=bk: h.activation(out=gv[n][:], in_=bk[:], func=AF.Gelu_apprx_tanh),
                         reads=[bk], writes=[gv[n]])
                    k.op("dve", lambda h, n=n: h.bn_stats(out=stats[:, n, :], in_=gv[n][:]), reads=[gv[n]], writes=[stats])
                    k.op("dve", lambda h, n=n: h.bn_aggr(out=mv[:, n, :], in_=stats[:, n, :]), reads=[stats], writes=[mv])
                    lnst["v_left"] -= 1
                    if n == ns - 1:
                        release(uv[0])

                def u_task(m, s, tw, last):
                    uu = uvu["u"]
                    bk = banks.next()
                    mm_group(bk[:, :tw], [(uu[2][:, kk, m * 128:(m + 1) * 128], xn[s][kk][:, :tw]) for kk in range(NCH)],
                             [uu[1]] + xn[s], bk)
                    k.op("act", lambda h, bk=bk, s=s, m=m, tw=tw: h.activation(
                        out=ub[s][m][:, :tw], in_=bk[:, :tw], func=AF.Gelu_apprx_tanh), reads=[bk], writes=[ub[s][m]])
                    if last:
                        release(uu[0])

                lnst = {"act": False, "dve": False, "v_left": ns}

                def ln_act():
                    k.op("act", lambda h: h.activation(out=lnr[:, :ns], in_=mv[:, :ns, 1], func=AF.Ln, bias=EPS),
                         reads=[mv], writes=[lnr])
                    k.op("act", lambda h: h.activation(out=lnr[:, :ns], in_=lnr[:, :ns], func=AF.Exp, scale=-0.5),
                         reads=[lnr], writes=[lnr])

                def ln_dve():
                    for n, (s, kind, j) in enumerate(subs):
                        k.op("dve", lambda h, n=n: h.scalar_tensor_tensor(
                            out=gv[n][:], in0=gv[n][:], scalar=mv[:, n, 0:1], in1=lnp[0][:],
                            op0=ALU.subtract, op1=ALU.mult), reads=[gv[n], mv, lnp[0]], writes=[gv[n]])
                        if kind == "p":
                            k.op("dve", lambda h, n=n: h.scalar_tensor_tensor(
                                out=vnb[n][:], in0=gv[n][:], scalar=lnr[:, n:n + 1], in1=lnp[1][:],
                                op0=ALU.mult, op1=ALU.add), reads=[gv[n], lnr, lnp[1]], writes=[vnb[n]])
                        else:
                            k.op("dve", lambda h, n=n: h.scalar_tensor_tensor(
                                out=vo[:], in0=gv[n][:], scalar=lnr[:, n:n + 1], in1=lnp[1][:],
                                op0=ALU.mult, op1=ALU.add), reads=[gv[n], lnr, lnp[1]], writes=[vo])
                            k.op("act", lambda h, n=n: h.copy(out=vnb[n][:], in_=vo[:]), reads=[vo], writes=[vnb[n]])
                            out_toks.append(k.dma("sp", vo.dsrc, lambda h: h.dma_start(out=nvs_d[l], in_=vo[:]), reads=[vo]))
                    lnst["dve"] = True

                def sg_task(hp, s, kind, tw):
                    assert lnst["dve"]
                    ki = 0 if kind == "p" else 1
                    bk = banks.next()
                    mysubs = [(n, j) for n, (s2, _k, j) in enumerate(subs) if s2 == s]

                    def fn(h, bk=bk, mysubs=mysubs, ki=ki, hp=hp):
                        ins = None
                        for n, j in mysubs:
                            cols = slice(j * 128, (j + 1) * 128)
                            for hh_ in range(2):
                                hd = 2 * hp + hh_
                                o = bk[64 * hh_:64 * hh_ + 64, cols]
                                h.matmul(o, lhsT=vnb[n][:, hd * 64:(hd + 1) * 64], rhs=swT[l][ki][:, hd, :],
                                         start=True, stop=False)
                                ins = h.matmul(o, lhsT=sel[:, hd, :], rhs=bhl[l][ki][:], start=False, stop=True)
                        return ins
                    k.op("pe", fn, reads=[vnb[n] for n, _ in mysubs] + [swT[l][ki], sel, bhl[l][ki]],
                         writes=[bk])
                    k.op("dve", lambda h, bk=bk, s=s, hp=hp, tw=tw: h.tensor_tensor(
                        out=mix[s][hp][:, :tw], in0=ub[s][hp][:, :tw], in1=bk[:, :tw], op=ALU.mult),
                        reads=[ub[s][hp], bk], writes=[mix[s][hp]])

                extras = [(lambda n=n: v_task(n)) for n in range(ns)]
                ul = [(m, s, tw) for m in range(4) for s, (kind, tok0, tw) in tl]
                for i_, (m, s, tw) in enumerate(ul):
                    extras.append(lambda m=m, s=s, tw=tw, last=(i_ == len(ul) - 1): u_task(m, s, tw, last))
                for hp in range(4):
                    for s, (kind, tok0, tw) in tl:
                        extras.append(lambda hp=hp, s=s, kind=kind, tw=tw: sg_task(hp, s, kind, tw))

                if len(tl) == 1:
                    s0, (kind0, tok00, tw0) = tl[0]
                    pairs = [[(2 * i, s0, kind0, tok00, tw0), (2 * i + 1, s0, kind0, tok00, tw0)] for i in range(4)]
                else:
                    pairs = [[(c, s, kind, tok0, tw) for s, (kind, tok0, tw) in tl] for c in range(NCH)]
                per_it = -(-(len(extras) + 1) // len(pairs))
                sched = [4, 4, 3, 2] if len(tl) == 1 else [per_it] * len(pairs)
                uxs[0] = get_unit(("in", l, 2))
                if gi == 0 and l == 0:
                    mask_prep()
                grp0 = []
                if len(tl) == 1:
                    for (c, s, kind, tok0, tw) in pairs[0]:
                        bk = banks.next()
                        pre_xb[(c, s)] = bk
                        grp0.append((uxs[0], c, s, tw, bk))
                    fresh[tl[0][0]] = 0
                for kk in range(NCH):
                    for (un, c, s, tw, bk) in grp0:
                        k.op("pe", lambda h, un=un, c=c, s=s, tw=tw, bk=bk, kk=kk: h.matmul(
                            bk[:, :tw], lhsT=un[2][:, kk, (c % 4) * 128:(c % 4 + 1) * 128], rhs=xn[s][kk][:, :tw],
                            start=(kk == 0), stop=(kk == NCH - 1)), reads=[un[1], xn[s][kk]], writes=[bk])
                stage_a(pairs[0])
                flush_casts()
                for c in sorted(set(it[0] for it in pairs[0])):
                    yb_task(c)
                for i, items in enumerate(pairs):
                    if i + 1 < len(pairs):
                        stage_a(pairs[i + 1])
                    sb_ = b_pe(items)
                    b_rest(sb_)
                    if lnst["v_left"] == 0 and not lnst["act"]:
                        ln_act()
                        lnst["act"] = True
                        extras.insert(0, ln_dve)
                    flush_casts()
                    if i + 1 < len(pairs):
                        for c in sorted(set(it[0] for it in pairs[i + 1])):
                            yb_task(c)
                    if i == 0:
                        uvu["v"] = get_unit(("in", l, 1))
                        uvu["u"] = get_unit(("in", l, 0))
                    for _ in range(sched[i]):
                        if extras:
                            extras.pop(0)()
                assert lnst["act"]
                while extras:
                    extras.pop(0)()
                early_k = [4, 5, 6, 7, 8, 9]
                late_k = [0, 1, 2, 3, 10, 11]
                uo0 = get_unit(("out", l, 0))
                uo1 = get_unit(("out", l, 1))
                first = []
                for m in range(4):
                    uo = uo0 if m < 2 else uo1
                    for s, (kind, tok0, tw) in tl:
                        first.append((m, s, tw, uo, banks.next()))
                for phase, ks in enumerate((early_k, late_k)):
                    for (m, s, tw, uo, bk) in first:
                        for i_, kk in enumerate(ks):
                            k.op("pe", lambda h, bk=bk, uo=uo, m=m, s=s, tw=tw, kk=kk, st=(phase == 0 and i_ == 0),
                                 sp_=(phase == 1 and i_ == len(ks) - 1): h.matmul(
                                     bk[:, :tw], lhsT=uo[2][:, kk, (m % 2) * 128:(m % 2 + 1) * 128], rhs=mix[s][kk][:, :tw],
                                     start=st, stop=sp_), reads=[uo[1], mix[s][kk]], writes=[bk])
                for (m, s, tw, uo, bk) in first:
                    k.op("dve", lambda h, bk=bk, s=s, m=m, tw=tw: h.tensor_tensor(
                        out=x[s][m][:, :tw], in0=x[s][m][:, :tw], in1=bk[:, :tw], op=ALU.add),
                        reads=[x[s][m], bk], writes=[x[s][m]])
                release(uo0[0])
                release(uo1[0])
                for ob in range(2, 4):
                    uo = get_unit(("out", l, ob))
                    for mm_ in range(2):
                        m = ob * 2 + mm_
                        for s, (kind, tok0, tw) in tl:
                            bk = banks.next()
                            mm_group(bk[:, :tw], [(uo[2][:, kk, mm_ * 128:(mm_ + 1) * 128], mix[s][kk][:, :tw]) for kk in range(12)],
                                     [uo[1]] + mix[s], bk)
                            k.op("dve", lambda h, bk=bk, s=s, m=m, tw=tw: h.tensor_tensor(
                                out=x[s][m][:, :tw], in0=x[s][m][:, :tw], in1=bk[:, :tw], op=ALU.add),
                                reads=[x[s][m], bk], writes=[x[s][m]])
                    release(uo[0])
                for s, (kind, tok0, tw) in tl:
                    if kind == "s":
                        out_toks.append(k.dma("sp", osrc, lambda h: h.dma_start(out=nhs_d[l], in_=hs_stage[l][:]),
                                              reads=[hs_stage[l]]))
                        out_toks.append(k.dma("sp", osrc, lambda h: h.dma_start(out=ncs_d[l], in_=cs_stage[l][:]),
                                              reads=[cs_stage[l]]))
                    elif tok0 + tw == SEQ:
                        out_toks.append(k.dma("sp", osrc, lambda h: h.dma_start(out=nhp_d[l], in_=hstate[l][:]),
                                              reads=[hstate[l]]))
                        out_toks.append(k.dma("sp", osrc, lambda h: h.dma_start(out=ncp_d[l], in_=hist[l][:]),
                                              reads=[hist[l]]))
                for s, (kind, tok0, tw) in tl:
                    rms_norm(s, tw, vb + V_NMLP, "xn")
                ahead = (l == DEPTH - 1 and nxt is not None)
                if ahead:
                    alt = gv[0:4] + xcb_.items[0:4]
                    nkind, ntok0, ntw = nxt
                    for c_ in range(NCH):
                        k.dma("sp", xsrc[0][c_], lambda h, c_=c_: h.dma_start(out=alt[c_][:, :ntw], in_=xp_d[:, c_, ntok0:ntok0 + ntw]),
                              writes=[alt[c_]])
                    nxt_s = len(groups[gi + 1]) > 1
                    if nxt_s:
                        for c_ in range(NCH):
                            k.dma("sp", xsrc[1][c_], lambda h, c_=c_: h.dma_start(out=x[1][c_][:, :128], in_=xs_d[:, c_, :]),
                                  writes=[x[1][c_]])
                for q in range(4):
                    for half in range(2):
                        u1 = get_unit(("w1", l, 2 * q + half))
                        pre = {}
                        if q == 0 and half == 0:
                            for s, (kind, tok0, tw) in tl:
                                fresh[s] = 0
                                bks = [banks.next() for _ in range(4)]
                                for kk in range(NCH):
                                    for jj in range(4):
                                        k.op("pe", lambda h, kk=kk, jj=jj, s=s, tw=tw, bks=bks: h.matmul(
                                            bks[jj][:, :tw], lhsT=u1[2][:, kk, jj * 128:(jj + 1) * 128], rhs=xn[s][kk][:, :tw],
                                            start=(kk == 0), stop=(kk == NCH - 1)),
                                            reads=[u1[1], xn[s][kk]], writes=[bks[jj]])
                                for jj in range(4):
                                    pre[(jj, s)] = bks[jj]
                        for jj in range(4):
                            j = half * 4 + jj
                            for s, (kind, tok0, tw) in tl:
                                if (jj, s) in pre:
                                    bk = pre[(jj, s)]
                                else:
                                    bk = banks.next()
                                    mm_xn(bk[:, :tw], u1[2], u1[1], jj * 128, s, tw, bk)
                                hj = hid[s][j]
                                k.op("act", lambda h, bk=bk, hj=hj, tw=tw: h.activation(out=hj[:, :tw], in_=bk[:, :tw], func=AF.Relu),
                                     reads=[bk], writes=[hj])
                                k.op("dve", lambda h, hj=hj, tw=tw: h.tensor_tensor(out=hj[:, :tw], in0=hj[:, :tw], in1=hj[:, :tw], op=ALU.mult),
                                     reads=[hj], writes=[hj])
                        release(u1[0])
                    u2a = get_unit(("w2", l, 2 * q))
                    u2b = get_unit(("w2", l, 2 * q + 1))
                    for m in range(NCH):
                        if ahead and q == 3 and m == 4:
                            rms_norm(0, ntw, V_NMIX, "xn", xt=alt)
                            if nxt_s:
                                rms_norm(1, 128, V_NMIX, "xn")
                        for s, (kind, tok0, tw) in tl:
                            bk = banks.next()
                            pairs = []
                            for kk in range(8):
                                uw = u2a if kk < 4 else u2b
                                pairs.append((uw[2][:, kk % 4, m * 128:(m + 1) * 128], hid[s][kk][:, :tw]))
                            if m == 0:
                                mm_group(bk[:, :tw], pairs, None, bk,
                                         split_reads=[[(u2a if kk < 4 else u2b)[1], hid[s][kk]] for kk in range(8)])
                            else:
                                mm_group(bk[:, :tw], pairs, [u2a[1], u2b[1]] + hid[s], bk)
                            k.op("dve", lambda h, bk=bk, s=s, m=m, tw=tw: h.tensor_tensor(
                                out=x[s][m][:, :tw], in0=x[s][m][:, :tw], in1=bk[:, :tw], op=ALU.add),
                                reads=[x[s][m], bk], writes=[x[s][m]])
                    release(u2a[0])
                    release(u2b[0])
            for s, ti in enumerate(g):
                kind, tok0, tw = TILES[ti]
                rms_norm(s, tw, V_FIN, "y", tile_info=TILES[ti])
            if nxt is not None:
                old_x = x[0]
                x[0] = alt
                for i_ in range(4):
                    gv[i_] = old_x[4 + i_]
                xcb_.items = old_x[0:4]
                xcb_.i = 0
                prenormed = True
        k.wait_all("sp", out_toks)
    return nc


def _fm(a):
    a = np.asarray(a, np.float32)
    lead = a.shape[:-1]
    return np.ascontiguousarray(np.moveaxis(a.reshape(lead + (NCH, 128)), -1, 0))


_NC_CACHE = {}


def kernel(x_prompt, x_sample, state_lru_h, state_conv, norm_mix_g, w_in, conv_w, conv_b,
           gate_r_w, gate_r_b, gate_i_w, gate_i_b, lru_lambda, sgu_norm_g, sgu_norm_b, sgu_w, sgu_b,
           w_out, norm_mlp_g, mlp_w1, mlp_w2, final_norm_g):
    f = lambda a: np.ascontiguousarray(np.asarray(a, np.float32))
    x_prompt, x_sample, state_lru_h, state_conv = map(f, (x_prompt, x_sample, state_lru_h, state_conv))
    w_in, w_out, mlp_w1, mlp_w2 = map(f, (w_in, w_out, mlp_w1, mlp_w2))
    sgu_w = f(sgu_w); sgu_b = f(sgu_b)

    vecs = np.zeros((128, NV), np.float32)
    for l in range(DEPTH):
        b = l * LV
        vecs[:, b + V_NMIX:b + V_NMIX + 8] = _fm(norm_mix_g[l])
        vecs[:, b + V_NMLP:b + V_NMLP + 8] = _fm(norm_mlp_g[l])
        for kk in range(4):
            vecs[:, b + V_CW + 8 * kk:b + V_CW + 8 * kk + 8] = _fm(np.asarray(conv_w)[l, kk])
        vecs[:, b + V_CB:b + V_CB + 8] = _fm(conv_b[l])
        vecs[:, b + V_RB:b + V_RB + 8] = _fm(gate_r_b[l])
        vecs[:, b + V_IB:b + V_IB + 8] = _fm(gate_i_b[l])
        vecs[:, b + V_LAM:b + V_LAM + 8] = _fm(lru_lambda[l])
    vecs[:, V_FIN:V_FIN + 8] = _fm(final_norm_g)

    lnp = np.zeros((DEPTH, 2, 128, DSGU), np.float32)
    lnp[:, 0] = np.asarray(sgu_norm_g, np.float32)[:, None, :]
    lnp[:, 1] = np.asarray(sgu_norm_b, np.float32)[:, None, :]

    sw = np.zeros((DEPTH, 2, 128, 8, 128), np.float32)
    sw[:, 0] = np.transpose(sgu_w, (0, 3, 1, 2))
    blk = np.transpose(sgu_w[:, :, :8, :8], (0, 3, 1, 2))
    for b_ in range(16):
        sw[:, 1, 8 * b_:8 * b_ + 8, :, 8 * b_:8 * b_ + 8] = blk
    brow = np.zeros((DEPTH, 2, 40, 128), np.float32)
    for r0 in (0, 32):
        brow[:, 0, r0:r0 + 8] = sgu_b
        brow[:, 1, r0:r0 + 8] = np.tile(sgu_b[:, :, :8], (1, 1, 16))
    sel40 = np.zeros((40, 8, 64), np.float32)
    for h_ in range(8):
        sel40[h_, h_, :] = 1.0
        sel40[32 + h_, h_, :] = 1.0

    gbd = np.zeros((DEPTH, 2, 128, NCH, 128), np.float32)
    for gi, gw in enumerate((np.asarray(gate_r_w, np.float32), np.asarray(gate_i_w, np.float32))):
        for c in range(NCH):
            for bb in range(2):
                gbd[:, gi, 64 * bb:64 * bb + 64, c, 64 * bb:64 * bb + 64] = gw[:, 2 * c + bb]

    if "nc" not in _NC_CACHE:
        _NC_CACHE["nc"] = build_program()
    nc = _NC_CACHE["nc"]

    in_maps = []
    for core in range(NCORES):
        xp = np.ascontiguousarray(x_prompt[core].T.reshape(NCH, 128, SEQ).transpose(1, 0, 2))
        xs = x_sample[16 * core:16 * core + 16].reshape(128, D)
        xs = np.ascontiguousarray(xs.T.reshape(NCH, 128, 128).transpose(1, 0, 2))
        h0 = state_lru_h[:, 16 * core:16 * core + 16]
        h0s = np.ascontiguousarray(h0.reshape(DEPTH, 16, NCH, 128).transpose(0, 3, 2, 1))
        cv = state_conv[:, 16 * core:16 * core + 16]
        cv0s = np.ascontiguousarray(cv.reshape(DEPTH, 16, 3, NCH, 128).transpose(0, 4, 3, 1, 2))
        in_maps.append({
            "xp": xp, "xs": xs, "h0s": h0s, "cv0s": cv0s, "vecs": vecs, "lnp": lnp, "brow": brow, "sel40": sel40,
            "sw": sw, "gbd": gbd, "w_in": w_in, "w_out": w_out, "w1": mlp_w1, "w2": mlp_w2,
        })
    res = run_bass_kernel_spmd(nc, in_maps, core_ids=list(range(NCORES)))
    R = res.results

    y_prompt = np.stack([R[c]["yp"].transpose(1, 0, 2).reshape(D, SEQ).T for c in range(NCORES)])
    y_sample = np.concatenate([R[c]["ys"].transpose(1, 0, 2).reshape(D, 128).T.reshape(16, 8, D) for c in range(NCORES)])
    nhp = np.stack([R[c]["nhp"].transpose(0, 2, 1).reshape(DEPTH, D) for c in range(NCORES)], axis=1)
    ncp = np.stack([R[c]["ncp"].transpose(0, 3, 2, 1).reshape(DEPTH, 3, D) for c in range(NCORES)], axis=1)
    nhs = np.concatenate([R[c]["nhs"].transpose(0, 3, 2, 1).reshape(DEPTH, 16, D) for c in range(NCORES)], axis=1)
    ncs = np.concatenate([R[c]["ncs"].transpose(0, 3, 4, 2, 1).reshape(DEPTH, 16, 3, D) for c in range(NCORES)], axis=1)
    nvs = np.concatenate([R[c]["nvs"].reshape(DEPTH, 16, 8, DSGU) for c in range(NCORES)], axis=1)
    out = (y_prompt, y_sample, nhp, ncp, nhs, ncs, nvs)
    return tuple(np.ascontiguousarray(o, dtype=np.float32) for o in out)
```
